# Optimizing a Trainium2 kernel written in Bass

```python
import math
import jax, jax.numpy as jnp
from jax import lax
import numpy as np

D_MODEL = 1024
BATCH = 8
SEQ = 2048
DEPTH = 4
DEC_BATCH = 128
DEC_SEQ = 1
PAST_LEN = 16384
PAGE_SIZE = 128

N_MIXERS = 2
N_CHUNK_LAYERS = (DEPTH + N_MIXERS - 1) // N_MIXERS
N_DELTA_LAYERS = DEPTH // N_MIXERS
CHUNK = 128
D_A = 2 * D_MODEL
H_A = 8
HD_A = D_A // H_A
H_B = 8
DK = 128
DV = 128
KEY_DIM = H_B * DK
VAL_DIM = H_B * DV
QKV_DIM = 2 * KEY_DIM + VAL_DIM
B_IN_DIM = QKV_DIM + VAL_DIM + 2 * H_B
CONV_W = 4
DELTA_CHUNK = 64
D_FF = ((8 * D_MODEL + 3 * 256 - 1) // (3 * 256)) * 256
EPS = 1e-6

kernel_name = "hybrid_chunkmlp_gdn_decode_step"


def rmsnorm(x, g):
    xf = x.astype(jnp.float32)
    xf = xf * lax.rsqrt(jnp.mean(xf * xf, axis=-1, keepdims=True) + EPS)
    return (xf * g.astype(jnp.float32)).astype(x.dtype)


def l2norm(x):
    xf = x.astype(jnp.float32)
    return xf * lax.rsqrt(jnp.sum(xf * xf, axis=-1, keepdims=True) + EPS)


def swiglu_ffn(h, w_in, w_out):
    gate, up = jnp.split(h @ w_in, 2, axis=-1)
    return (jax.nn.silu(gate) * up) @ w_out


def chunk_spatial_mix(v, w_s, b_s):
    B, T, _ = v.shape
    nc = -(-T // CHUNK)
    pad = nc * CHUNK - T
    vp = jnp.pad(v, ((0, 0), (0, pad), (0, 0))).reshape(B, nc, CHUNK, H_A, HD_A)
    causal = jnp.tril(jnp.ones((CHUNK, CHUNK), dtype=bool))
    w = jnp.where(causal, w_s, jnp.zeros_like(w_s))
    out = jnp.einsum('gts,bcsgd->bctgd', w, vp) + b_s.T[None, None, :, :, None]
    return out.reshape(B, nc * CHUNK, D_A)[:, :T]


def chunk_mlp_mixer(h, w_in, g_v, w_s, b_s, w_out):
    u, v = jnp.split(jax.nn.gelu(h @ w_in), 2, axis=-1)
    v = rmsnorm(v, g_v)
    return (u * chunk_spatial_mix(v, w_s, b_s)) @ w_out, v


def short_conv(buf, x_new, w):
    full = jnp.concatenate([buf.astype(x_new.dtype), x_new], axis=1)
    T = x_new.shape[1]
    y = full[:, 0:T] * w[0]
    for j in range(1, CONV_W):
        y = y + full[:, j:j + T] * w[j]
    return jax.nn.silu(y), full[:, -(CONV_W - 1):]


def delta_recurrent(q, k, v, g, beta, s0):
    def step(S, inp):
        q_t, k_t, v_t, g_t, b_t = inp
        S = S * jnp.exp(g_t)[..., None, None]
        pred = jnp.einsum('bhkv,bhk->bhv', S, k_t)
        S = S + jnp.einsum('bhk,bhv->bhkv', k_t, b_t[..., None] * (v_t - pred))
        return S, jnp.einsum('bhkv,bhk->bhv', S, q_t)
    xs = (jnp.moveaxis(q, 1, 0), jnp.moveaxis(k, 1, 0), jnp.moveaxis(v, 1, 0),
          jnp.moveaxis(g, 1, 0), jnp.moveaxis(beta, 1, 0))
    S, o = lax.scan(step, s0, xs)
    return jnp.moveaxis(o, 0, 1), S


def delta_chunked(q, k, v, g, beta, s0):
    B, T, H, _ = q.shape
    C = DELTA_CHUNK
    nc = T // C

    def blk(a):
        a = a.reshape(B, nc, C, H, *a.shape[3:])
        return jnp.moveaxis(jnp.moveaxis(a, 1, 0), 2, 3)

    q, k, v, g, beta = blk(q), blk(k), blk(v), blk(g), blk(beta)
    gc = jnp.cumsum(g, axis=-1)
    idx = jnp.arange(C)
    incl = idx[:, None] >= idx[None, :]
    strict = idx[:, None] > idx[None, :]
    diff = gc[..., :, None] - gc[..., None, :]
    decay = jnp.where(incl, jnp.exp(jnp.where(incl, diff, 0.0)), 0.0)
    kb = k * beta[..., None]
    L = jnp.where(strict, jnp.einsum('...id,...jd->...ij', kb, k) * decay, 0.0)
    eye = jnp.eye(C, dtype=jnp.float32)
    Tm = lax.linalg.triangular_solve(eye + L, jnp.broadcast_to(eye, L.shape),
                                     left_side=True, lower=True)
    u = Tm @ (v * beta[..., None])
    w = Tm @ (kb * jnp.exp(gc)[..., None])
    A = jnp.einsum('...id,...jd->...ij', q, k) * decay

    def step(S, inp):
        q_c, k_c, u_c, w_c, gc_c, A_c = inp
        v_new = u_c - jnp.einsum('bhck,bhkv->bhcv', w_c, S)
        o = (jnp.einsum('bhck,bhkv->bhcv', q_c * jnp.exp(gc_c)[..., None], S)
             + jnp.einsum('bhij,bhjv->bhiv', A_c, v_new))
        g_last = gc_c[..., -1:]
        S = (S * jnp.exp(g_last)[..., None]
             + jnp.einsum('bhck,bhcv->bhkv', k_c * jnp.exp(g_last - gc_c)[..., None], v_new))
        return S, o

    S, o = lax.scan(step, s0, (q, k, u, w, gc, A))
    o = jnp.moveaxis(jnp.moveaxis(o, 3, 2), 0, 1).reshape(B, T, H, -1)
    return o, S


def delta_mixer(h, conv_buf, s0, w_in, w_conv, a_log, dt_bias, g_o, w_out, chunked):
    B, T, _ = h.shape
    proj = h @ w_in
    qkv_raw, gate, ba = jnp.split(proj, [QKV_DIM, QKV_DIM + VAL_DIM], axis=-1)
    qkv, new_buf = short_conv(conv_buf, qkv_raw, w_conv)
    q, k, v = jnp.split(qkv, [KEY_DIM, 2 * KEY_DIM], axis=-1)
    q = l2norm(q.reshape(B, T, H_B, DK)) * (DK ** -0.5)
    k = l2norm(k.reshape(B, T, H_B, DK))
    v = v.reshape(B, T, H_B, DV).astype(jnp.float32)
    b_raw, a_raw = jnp.split(ba.astype(jnp.float32), 2, axis=-1)
    beta = jax.nn.sigmoid(b_raw)
    g = -jnp.exp(a_log.astype(jnp.float32)) * jax.nn.softplus(a_raw + dt_bias.astype(jnp.float32))
    if chunked:
        o, s_new = delta_chunked(q, k, v, g, beta, s0.astype(jnp.float32))
    else:
        o, s_new = delta_recurrent(q, k, v, g, beta, s0.astype(jnp.float32))
    o = rmsnorm(o, g_o) * jax.nn.silu(gate.reshape(B, T, H_B, DV).astype(jnp.float32))
    return o.reshape(B, T, VAL_DIM).astype(h.dtype) @ w_out, new_buf, s_new


def run_trunk(x, conv_in, delta_in, p, chunked):
    v_rows, conv_out, delta_out = [], [], []
    for i in range(DEPTH):
        h = rmsnorm(x, p['norm_mix'][i])
        j = i // N_MIXERS
        if i % N_MIXERS == 0:
            y, v = chunk_mlp_mixer(h, p['a_w_in'][j], p['a_v_norm'][j], p['a_w_spatial'][j],
                                   p['a_b_spatial'][j], p['a_w_out'][j])
            v_rows.append(v)
        else:
            y, cb, s = delta_mixer(h, conv_in[j], delta_in[j], p['b_w_in'][j], p['b_w_conv'][j],
                                   p['b_a_log'][j], p['b_dt_bias'][j], p['b_o_norm'][j],
                                   p['b_w_out'][j], chunked)
            conv_out.append(cb)
            delta_out.append(s)
        x = x + y
        x = x + swiglu_ffn(rmsnorm(x, p['norm_ffn'][i]), p['ffn_w_in'][i], p['ffn_w_out'][i])
    return rmsnorm(x, p['norm_final']), v_rows, conv_out, delta_out


def setup_inputs(seed: int = 0) -> dict:
    key = jax.random.key(seed)
    ks = jax.random.split(key, 24)
    f32 = jnp.float32
    NA, NB = N_CHUNK_LAYERS, N_DELTA_LAYERS

    def nrm(k, shape, scale):
        return jax.random.normal(k, shape, f32) * scale

    def gain(k, shape):
        return 1.0 + 0.05 * jax.random.normal(k, shape, f32)

    dt = jnp.exp(jax.random.uniform(ks[16], (NB, H_B), f32, math.log(1e-3), math.log(1e-1)))
    return {
        'x_prompt': nrm(ks[0], (BATCH, SEQ, D_MODEL), 1.0),
        'x_sample': nrm(ks[1], (DEC_BATCH, DEC_SEQ, D_MODEL), 1.0),
        'state_delta': nrm(ks[2], (NB, DEC_BATCH, H_B, DK, DV), 0.1),
        'state_conv': nrm(ks[3], (NB, DEC_BATCH, CONV_W - 1, QKV_DIM), 1.0),
        'norm_mix': gain(ks[4], (DEPTH, D_MODEL)),
        'norm_ffn': gain(ks[5], (DEPTH, D_MODEL)),
        'norm_final': gain(ks[6], (D_MODEL,)),
        'a_w_in': nrm(ks[7], (NA, D_MODEL, 2 * D_A), D_MODEL ** -0.5),
        'a_v_norm': gain(ks[8], (NA, D_A)),
        'a_w_spatial': nrm(ks[9], (NA, H_A, CHUNK, CHUNK), 0.5 * CHUNK ** -0.5),
        'a_b_spatial': 1.0 + 0.1 * jax.random.normal(ks[10], (NA, H_A, CHUNK), f32),
        'a_w_out': nrm(ks[11], (NA, D_A, D_MODEL), 0.5 * D_A ** -0.5),
        'b_w_in': nrm(ks[12], (NB, D_MODEL, B_IN_DIM), D_MODEL ** -0.5),
        'b_w_conv': nrm(ks[13], (NB, CONV_W, QKV_DIM), CONV_W ** -0.5),
        'b_a_log': jnp.log(jax.random.uniform(ks[14], (NB, H_B), f32, 1.0, 16.0)),
        'b_dt_bias': dt + jnp.log(-jnp.expm1(-dt)),
        'b_o_norm': gain(ks[15], (NB, DV)),
        'b_w_out': nrm(ks[17], (NB, VAL_DIM, D_MODEL), 0.5 * VAL_DIM ** -0.5),
        'ffn_w_in': nrm(ks[18], (DEPTH, D_MODEL, 2 * D_FF), D_MODEL ** -0.5),
        'ffn_w_out': nrm(ks[19], (DEPTH, D_FF, D_MODEL), 0.5 * D_FF ** -0.5),
    }


def reference(x_prompt, x_sample, state_delta, state_conv, norm_mix, norm_ffn, norm_final,
              a_w_in, a_v_norm, a_w_spatial, a_b_spatial, a_w_out,
              b_w_in, b_w_conv, b_a_log, b_dt_bias, b_o_norm, b_w_out,
              ffn_w_in, ffn_w_out):
    p = {'norm_mix': norm_mix, 'norm_ffn': norm_ffn, 'norm_final': norm_final,
         'a_w_in': a_w_in, 'a_v_norm': a_v_norm, 'a_w_spatial': a_w_spatial,
         'a_b_spatial': a_b_spatial, 'a_w_out': a_w_out,
         'b_w_in': b_w_in, 'b_w_conv': b_w_conv, 'b_a_log': b_a_log, 'b_dt_bias': b_dt_bias,
         'b_o_norm': b_o_norm, 'b_w_out': b_w_out,
         'ffn_w_in': ffn_w_in, 'ffn_w_out': ffn_w_out}
    conv0 = jnp.zeros((N_DELTA_LAYERS, BATCH, CONV_W - 1, QKV_DIM), x_prompt.dtype)
    delta0 = jnp.zeros((N_DELTA_LAYERS, BATCH, H_B, DK, DV), jnp.float32)
    y_prompt, _, conv_p, delta_p = run_trunk(x_prompt, conv0, delta0, p, True)
    y_sample, v_s, conv_s, delta_s = run_trunk(x_sample, state_conv, state_delta, p, False)
    new_delta_prompt = jnp.stack(delta_p)
    new_conv_prompt = jnp.stack(conv_p)
    new_delta_sample = jnp.stack(delta_s)
    new_conv_sample = jnp.stack(conv_s)
    new_chunk_v_sample = jnp.stack(v_s)
    return (y_prompt, y_sample, new_delta_prompt, new_conv_prompt,
            new_delta_sample, new_conv_sample, new_chunk_v_sample)
```

```python
import contextlib
import numpy as np
import concourse.bass as bass
import concourse.mybir as mybir
from concourse.bass_utils import run_bass_kernel_spmd

F32 = mybir.dt.float32
BF16 = mybir.dt.bfloat16
AF = mybir.ActivationFunctionType
ALU = mybir.AluOpType
AX = mybir.AxisListType

NCORES = 8
D = 1024
KC = 8
SEQ = 2048
NS = 16
NT = SEQ + NS
DEPTH = 4
D_A = 2048
H_A = 8
D_FF = 2816
H_B = 8
QKV = 3072
B_IN = 4112
EPS = 1e-6
TILES = [(0, 512), (512, 512), (1024, 512), (1536, 512), (2048, NS)]
ESZ = {F32: 4, BF16: 2}


class V:
    __slots__ = ("ap", "space", "lo", "hi")

    def __init__(self, ap, space, lo, hi):
        self.ap, self.space, self.lo, self.hi = ap, space, lo, hi


class Region:
    def __init__(self, base, space, byte_lo, dtype, shape, parts=128):
        self.space, self.lo, self.dtype, self.shape, self.parts = space, byte_lo, dtype, tuple(shape), parts
        es = ESZ[dtype]
        n = int(np.prod(shape))
        self.nbytes = n * es
        assert byte_lo % 4 == 0
        ap = base[0:parts, byte_lo // 4:(byte_lo + self.nbytes + 3) // 4]
        if dtype != F32:
            ap = ap.bitcast(dtype)
        if len(shape) > 1:
            names = " ".join("d%d" % i for i in range(len(shape)))
            kw = {"d%d" % i: shape[i] for i in range(1, len(shape))}
            ap = ap.rearrange("p (%s) -> p %s" % (names, names), **kw)
        self.ap = ap
        st = [1] * len(shape)
        for i in range(len(shape) - 2, -1, -1):
            st[i] = st[i + 1] * shape[i + 1]
        self.strides = st

    def __getitem__(self, idx):
        if not isinstance(idx, tuple):
            idx = (idx,)
        idx = idx + (slice(None),) * (1 + len(self.shape) - len(idx))
        ap = self.ap[idx]
        es = ESZ[self.dtype]
        lo = 0
        hi = 0
        for i, ix in enumerate(idx[1:]):
            if isinstance(ix, slice):
                a = 0 if ix.start is None else ix.start
                b = self.shape[i] if ix.stop is None else ix.stop
                assert ix.step in (None, 1) and 0 <= a < b <= self.shape[i], (ix, self.shape)
            else:
                a, b = ix, ix + 1
                assert 0 <= a < self.shape[i]
            lo += a * self.strides[i]
            hi += (b - 1) * self.strides[i]
        blo, bhi = self.lo + lo * es, self.lo + (hi + 1) * es
        if self.space == "P":
            blo = blo // 2048 * 2048
            bhi = (bhi + 2047) // 2048 * 2048
        return V(ap, self.space, blo, bhi)

    def full(self):
        return self[(slice(None),)]


class Prog:
    def __init__(self, nc, es, arena_bytes=207 * 1024):
        self.nc = nc
        self.es = es
        self.arena = es.enter_context(nc.sbuf_tensor("arena", [128, arena_bytes // 4], F32))
        self.psum = es.enter_context(nc.psum_tensor("psum", [128, 4096], F32))
        self.arena_bytes = arena_bytes
        self.top = 0
        self.eng = {"pe": nc.tensor, "act": nc.scalar, "dve": nc.vector, "pool": nc.gpsimd, "sp": nc.sync}
        self.sem = {e: es.enter_context(nc.semaphore("c_" + e)) for e in self.eng}
        self.cnt = {e: 0 for e in self.eng}
        self.NDS = 8
        self.dsem = {q: [es.enter_context(nc.semaphore("d_%s%d" % (q, i))) for i in range(self.NDS)]
                     for q in ("sp", "pool")}
        self.dn = {q: 0 for q in ("sp", "pool")}
        self.waited = {e: {} for e in self.eng}
        self.recs = {"S": [], "P": []}
        self.bank_rr = 0
        self.nwaits = 0
        self.nops = 0

    def alloc(self, dtype, shape, parts=128):
        n = int(np.prod(shape)) * ESZ[dtype]
        n = (n + 31) // 32 * 32
        lo = self.top
        self.top += n
        assert self.top <= self.arena_bytes, ("SBUF arena overflow", self.top)
        return Region(self.arena, "S", lo, dtype, shape, parts)

    def mark(self):
        return self.top

    def release(self, m):
        self.top = m

    def bank(self, dtype=F32, shape=None, parts=128, b=None):
        if b is None:
            b = self.bank_rr
            self.bank_rr = (self.bank_rr + 1) % 8
        if shape is None:
            shape = (2048 // ESZ[dtype],)
        return Region(self.psum, "P", b * 2048, dtype, shape, parts)

    def bank2(self, dtype=F32, shape=None, parts=128):
        if self.bank_rr % 2:
            self.bank_rr = (self.bank_rr + 1) % 8
        b = self.bank_rr
        self.bank_rr = (self.bank_rr + 2) % 8
        if shape is None:
            shape = (4096 // ESZ[dtype],)
        return Region(self.psum, "P", b * 2048, dtype, shape, parts)

    def _collect(self, reads, writes, me=None):
        deps = {}
        for lst, isw in ((reads, False), (writes, True)):
            for v in lst:
                isp = v.space == "P"
                for r in self.recs[v.space]:
                    if r[0] < v.hi and v.lo < r[1] and (isw or r[4] or (isp and r[2] != me)):
                        k = r[2]
                        if deps.get(k, (None, 0))[1] < r[3]:
                            deps[k] = (r[5], r[3])
        return deps

    def _record(self, reads, writes, semkey, sem, val):
        for v in writes:
            rl = self.recs[v.space]
            rl[:] = [r for r in rl if not (v.lo <= r[0] and r[1] <= v.hi)]
            rl.append([v.lo, v.hi, semkey, val, True, sem])
        for v in reads:
            rl = self.recs[v.space]
            for r in rl:
                if r[0] == v.lo and r[1] == v.hi and r[2] == semkey and not r[4]:
                    r[3] = val
                    break
            else:
                rl.append([v.lo, v.hi, semkey, val, False, sem])

    def _waits(self, e, deps, skip_self=False):
        w = self.waited[e]
        for k, (sem, val) in deps.items():
            if skip_self and k == e:
                continue
            if w.get(k, 0) < val:
                self.eng[e].wait_ge(sem, val)
                w[k] = val
                self.nwaits += 1

    def op(self, e, reads, writes, fn):
        deps = self._collect(reads, writes, e)
        self._waits(e, deps, skip_self=(e == "pe"))
        ins = fn(self.eng[e])
        self.cnt[e] += 1
        ins.then_inc(self.sem[e], 1)
        self._record(reads, writes, e, self.sem[e], self.cnt[e])
        self.nops += 1

    def dma(self, q, out, in_, out_v=None, in_v=None):
        reads, writes = [], []
        if isinstance(in_, V):
            reads.append(in_)
            in_ = in_.ap
        if isinstance(out, V):
            writes.append(out)
            out = out.ap
        n = self.dn[q]
        s = self.dsem[q][n % self.NDS]
        tgt = 16 * (n // self.NDS + 1)
        self.dn[q] += 1
        key = "%s_d%d" % (q, n % self.NDS)
        deps = self._collect(reads, writes)
        if tgt > 16:
            deps[key] = (s, max(deps.get(key, (None, 0))[1], tgt - 16))
        self._waits(q, deps)
        self.eng[q].dma_start(out=out, in_=in_).then_inc(s, 16)
        self._record(reads, writes, key, s, tgt)

    def finish(self):
        for q in ("sp", "pool"):
            for i in range(self.NDS):
                n = self.dn[q]
                cnt = n // self.NDS + (1 if i < n % self.NDS else 0)
                if cnt:
                    self.nc.sync.wait_ge(self.dsem[q][i], 16 * cnt)

    def mm(self, out, pairs, extra_reads=()):
        reads = list(extra_reads)
        for l, r in pairs:
            reads += [l, r]
        n = len(pairs)

        def fn(pe):
            ins = None
            for i, (l, r) in enumerate(pairs):
                ins = pe.matmul(out.ap, l.ap, r.ap, start=(i == 0), stop=(i == n - 1))
            return ins
        self.op("pe", reads, [out], fn)

    def transpose(self, out, in_, ident):
        self.op("pe", [in_, ident], [out], lambda pe: pe.matmul(out.ap, in_.ap, ident.ap, start=True, stop=True))

    def transpose_hw(self, out, in_, ident):
        self.op("pe", [in_, ident], [out], lambda pe: pe.transpose(out.ap, in_.ap, ident.ap))

    def act(self, out, in_, func, bias=None, scale=1.0, accum=None, eng="act"):
        reads = [in_]
        kw = {}
        if bias is not None:
            if isinstance(bias, V):
                reads.append(bias)
                kw["bias"] = bias.ap
            else:
                kw["bias"] = bias
        if isinstance(scale, V):
            reads.append(scale)
            kw["scale"] = scale.ap
        else:
            kw["scale"] = scale
        writes = [out]
        if accum is not None:
            writes.append(accum)
            kw["accum_out"] = accum.ap
        self.op("act", reads, writes, lambda a: a.activation(out=out.ap, in_=in_.ap, func=func, **kw))

    def tt(self, out, a, b, op, eng="dve"):
        self.op(eng, [a, b], [out], lambda e: e.tensor_tensor(out=out.ap, in0=a.ap, in1=b.ap, op=op))

    def ts(self, out, a, s1, op0, s2=None, op1=None, eng="dve"):
        reads = [a]
        s1a, s2a = s1, s2
        if isinstance(s1, V):
            reads.append(s1)
            s1a = s1.ap
        if isinstance(s2, V):
            reads.append(s2)
            s2a = s2.ap
        kw = {}
        if op1 is not None:
            kw["op1"] = op1
        self.op(eng, reads, [out],
                lambda e: e.tensor_scalar(out=out.ap, in0=a.ap, scalar1=s1a, scalar2=s2a, op0=op0, **kw))

    def stt(self, out, a, s, b, op0, op1, eng="dve"):
        reads = [a, b]
        sa = s
        if isinstance(s, V):
            reads.append(s)
            sa = s.ap
        self.op(eng, reads, [out],
                lambda e: e.scalar_tensor_tensor(out=out.ap, in0=a.ap, scalar=sa, in1=b.ap, op0=op0, op1=op1))

    def copy(self, out, in_, eng="dve"):
        if eng == "act":
            self.op("act", [in_], [out], lambda a: a.copy(out=out.ap, in_=in_.ap))
        else:
            self.op(eng, [in_], [out], lambda e: e.tensor_copy(out=out.ap, in_=in_.ap))

    def memset(self, out, val, eng="dve"):
        self.op(eng, [], [out], lambda e: e.memset(out.ap, val))

    def recip(self, out, in_):
        self.op("dve", [in_], [out], lambda e: e.reciprocal(out=out.ap, in_=in_.ap))


def build_program(cfg):
    nc = bass.Bass("TRN2", target_bir_lowering=False)
    es = contextlib.ExitStack()
    with es:
        _emit(nc, es, cfg)
    return nc


def _emit(nc, es, cfg):
    P = Prog(nc, es)
    layers = cfg.get("layers", ["A", "F", "B", "F", "A", "F", "B", "F"])

    def din(name, shape):
        return nc.dram_tensor(name, list(shape), F32, kind="ExternalInput").ap()

    def dout(name, shape):
        return nc.dram_tensor(name, list(shape), F32, kind="ExternalOutput").ap()

    xp = din("xp", (SEQ, D))
    xs = din("xs", (NS, D))
    sd = din("sd", (2, NS, H_B, 128, 128))
    sc = din("sc", (2, NS, 3, QKV))
    norm_mix = din("norm_mix", (DEPTH, D))
    norm_ffn = din("norm_ffn", (DEPTH, D))
    norm_final = din("norm_final", (D,))
    a_w_in = din("a_w_in", (2, D, 2 * D_A))
    a_v_norm = din("a_v_norm", (2, D_A))
    a_w_spatial = din("a_w_spatial", (2, H_A, 128, 128))
    a_b_spatial = din("a_b_spatial", (2, H_A, 128))
    a_w_out = din("a_w_out", (2, D_A, D))
    b_w_in = din("b_w_in", (2, D, B_IN))
    b_w_conv = din("b_w_conv", (2, 4, QKV))
    b_a_log = din("b_a_log", (2, H_B))
    b_dt_bias = din("b_dt_bias", (2, H_B))
    b_o_norm = din("b_o_norm", (2, 128))
    b_w_out = din("b_w_out", (2, D, D))
    ffn_w_in = din("ffn_w_in", (DEPTH, D, 2 * D_FF))
    ffn_w_out = din("ffn_w_out", (DEPTH, D_FF, D))

    yp = dout("yp", (SEQ, D))
    ys = dout("ys", (NS, D))
    ndp = dout("ndp", (2, H_B, 128, 128))
    ncp = dout("ncp", (2, 3, QKV))
    nds = dout("nds", (2, NS, H_B, 128, 128))
    ncs = dout("ncs", (2, NS, 3, QKV))
    ncv = dout("ncv", (2, NS, D_A))

    xT = P.alloc(F32, (KC, NT))
    ident_f = P.alloc(F32, (128,))
    ident_b = P.alloc(BF16, (128,))
    ones_b = P.alloc(BF16, (128,))
    ones128_b = P.alloc(BF16, (128,))
    ones_f = P.alloc(F32, (128,))
    epsc = P.alloc(F32, (1,))
    gmix = P.alloc(F32, (DEPTH, KC))
    gffn = P.alloc(F32, (DEPTH, KC))
    NW = 2
    WSLOT = [P.alloc(BF16, (KC, 512)) for _ in range(NW)]
    OSLOT = [P.alloc(BF16, (4, D)) for _ in range(2)]
    wctr = [0]
    octr = [0]

    P.memset(ones_b.full(), 1.0 / D)
    P.memset(ones128_b.full(), 1.0 / 128)
    P.memset(ones_f.full(), 1.0)
    P.memset(epsc.full(), EPS)
    P.memset(ident_f.full(), 0.0)
    P.op("pool", [ones_f.full()], [ident_f.full()],
         lambda g: g.affine_select(out=ident_f.ap, in_=ones_f.ap, pattern=[[-1, 128]], compare_op=ALU.is_equal,
                                   fill=0.0, base=0, channel_multiplier=1))
    P.copy(ident_b.full(), ident_f.full())
    with nc.allow_non_contiguous_dma(reason="tiny gain vectors"):
        P.dma("sp", gmix.full(), norm_mix.rearrange("l (k p) -> p l k", p=128))
        P.dma("sp", gffn.full(), norm_ffn.rearrange("l (k p) -> p l k", p=128))

    def load_w(dram_ap_fn):
        slot = WSLOT[wctr[0] % NW]
        wctr[0] += 1
        dram_ap_fn(slot)
        return slot

    def rmsnorm_T(hT, gain, li, sq, rs):
        for (c0, n) in TILES:
            for k in range(KC):
                P.act(sq[:, k, 0:n], xT[:, k, c0:c0 + n], AF.Square)
            ps = P.bank()
            P.mm(ps[:, 0:n], [(ones_b.full(), sq[:, k, 0:n]) for k in range(KC)])
            P.act(rs[:, 1, 0:n], ps[:, 0:n], AF.Ln, bias=epsc[:, 0:1])
            P.act(rs[:, 1, 0:n], rs[:, 1, 0:n], AF.Exp, scale=-0.5)
            for k in range(KC):
                P.stt(hT[:, k, c0:c0 + n], xT[:, k, c0:c0 + n], gain[:, li, k:k + 1], rs[:, 1, 0:n],
                      ALU.mult, ALU.mult)

    def load_x():
        m = P.mark()
        xin = [P.alloc(F32, (D,)) for _ in range(2)]
        nb = SEQ // 128
        for b in cfg.get('lx_blocks', range(nb + 1)):
            xi = xin[b % 2]
            rows = 128 if b < nb else NS
            src = xp[b * 128:(b + 1) * 128, :] if b < nb else xs
            P.dma("sp", xi[0:rows, :], src)
            for half in range(2):
                ps = P.bank()
                for kk in range(4):
                    k = half * 4 + kk
                    P.transpose_hw(ps[:, kk * 128:kk * 128 + rows], xi[0:rows, k * 128:(k + 1) * 128],
                                ident_f[0:rows, 0:rows])
                for kk in range(4):
                    k = half * 4 + kk
                    eng = "act" if (kk % 2 and not cfg.get("lx_noact")) else "dve"
                    P.copy(xT[:, k, b * 128:b * 128 + rows], ps[:, kk * 128:kk * 128 + rows], eng=eng)
        P.release(m)

    def final_out():
        m = P.mark()
        gfin = P.alloc(F32, (D,))
        P.dma("sp", gfin.full(), norm_final.partition_broadcast(128))
        sqt = P.alloc(F32, (D,))
        st = P.alloc(F32, (4,))
        yo = [P.alloc(F32, (D,)) for _ in range(2)]
        nb = SEQ // 128
        for b in range(nb + 1):
            rows = 128 if b < nb else NS
            ps = P.bank2()
            for k in range(KC):
                P.transpose_hw(ps[0:rows, k * 128:(k + 1) * 128], xT[:, k, b * 128:b * 128 + rows], ident_f.full())
            P.act(sqt[0:rows, :], ps[0:rows, :], AF.Square)
            P.op("dve", [sqt[0:rows, :]], [st[0:rows, 0:1]],
                 lambda e: e.reduce_sum(out=st[0:rows, 0:1].ap, in_=sqt[0:rows, :].ap, axis=AX.X))
            P.act(st[0:rows, 1:2], st[0:rows, 0:1], AF.Sqrt, bias=epsc[0:rows, 0:1], scale=1.0 / D)
            P.recip(st[0:rows, 2:3], st[0:rows, 1:2])
            y = yo[b % 2]
            P.stt(y[0:rows, :], ps[0:rows, :], st[0:rows, 2:3], gfin[0:rows, :], ALU.mult, ALU.mult)
            dst = yp[b * 128:(b + 1) * 128, :] if b < nb else ys
            P.dma("sp", dst, y[0:rows, :])
        P.release(m)

    def ffn(li):
        m = P.mark()
        WS = WSLOT + [P.alloc(BF16, (KC, 512))]
        hT = P.alloc(BF16, (KC, NT))
        sq = P.alloc(BF16, (KC, 512))
        rs = P.alloc(F32, (2, 512))
        rmsnorm_T(hT, gffn, li, sq, rs)
        actb = P.alloc(BF16, (4, NT))
        gs = [P.alloc(F32, (512,)) for _ in range(3)]
        gsc = 0
        win = ffn_w_in[li].rearrange("(k p) n -> p k n", p=128)
        wout = ffn_w_out[li].rearrange("(j p) n -> p j n", p=128)
        quarters = [(0, 4), (4, 4), (8, 4), (12, 4), (16, 4), (20, 2)]

        def issue_w(sb):
            slot = WS[wctr[0] % 3]
            wctr[0] += 1
            c = sb * 256
            P.dma("pool", slot[:, :, 0:256], win[:, :, c:c + 256])
            P.dma("pool", slot[:, :, 256:512], win[:, :, D_FF + c:D_FF + c + 256])
            return slot

        def issue_o(q0, nf):
            slot = OSLOT[octr[0] % 2]
            octr[0] += 1
            P.dma("pool", slot[:, 0:nf, :], wout[:, q0:q0 + nf, :])
            return slot

        sbs = [issue_w(0)]
        oslots = [issue_o(*quarters[0])]
        for qi, (q0, nf) in enumerate(quarters):
            for s in range(nf // 2):
                sb = q0 // 2 + s
                if sb + 1 < 11:
                    sbs.append(issue_w(sb + 1))
                w = sbs[sb]
                for j in range(2):
                    jj = s * 2 + j
                    for (c0, n) in TILES:
                        pg = P.bank()
                        pu = P.bank()
                        P.mm(pg[:, 0:n], [(w[:, k, j * 128:(j + 1) * 128], hT[:, k, c0:c0 + n]) for k in range(KC)])
                        P.mm(pu[:, 0:n], [(w[:, k, 256 + j * 128:256 + (j + 1) * 128], hT[:, k, c0:c0 + n])
                                          for k in range(KC)])
                        g = gs[gsc % 3]
                        gsc += 1
                        P.act(g[:, 0:n], pg[:, 0:n], AF.Silu)
                        P.tt(actb[:, jj, c0:c0 + n], g[:, 0:n], pu[:, 0:n], ALU.mult)
            if qi + 1 < len(quarters):
                oslots.append(issue_o(*quarters[qi + 1]))
            wo = oslots[qi]
            for (c0, n) in TILES:
                for d in range(KC):
                    ps = P.bank()
                    P.mm(ps[:, 0:n], [(wo[:, jj, d * 128:(d + 1) * 128], actb[:, jj, c0:c0 + n]) for jj in range(nf)])
                    P.tt(xT[:, d, c0:c0 + n], xT[:, d, c0:c0 + n], ps[:, 0:n], ALU.add)
        P.release(m)

    def rmsnorm_tile(hTt, gain, li, sq, rs, c0, n, o0=0):
        for k in range(KC):
            P.act(sq[:, k, 0:n], xT[:, k, c0:c0 + n], AF.Square)
        ps = P.bank()
        P.mm(ps[:, 0:n], [(ones_b.full(), sq[:, k, 0:n]) for k in range(KC)])
        P.act(rs[:, 0, 0:n], ps[:, 0:n], AF.Ln, bias=epsc[:, 0:1])
        P.act(rs[:, 0, 0:n], rs[:, 0, 0:n], AF.Exp, scale=-0.5)
        for k in range(KC):
            P.stt(hTt[:, k, o0:o0 + n], xT[:, k, c0:c0 + n], gain[:, li, k:k + 1], rs[:, 0, 0:n],
                  ALU.mult, ALU.mult)

    one_row = P.alloc(BF16, (128,))
    P.memset(one_row.full(), 1.0)

    def mixer_a(li, j):
        m = P.mark()
        WS = WSLOT + [P.alloc(BF16, (KC, 512))]
        NTT = 512 + NS
        hTt = P.alloc(BF16, (KC, NTT))
        rs = P.alloc(F32, (2, 512))
        uT = P.alloc(BF16, (16, NTT))
        sq = Region(P.arena, "S", uT.lo, BF16, (KC, 512))
        vtok = P.alloc(BF16, (4, D_A))
        vn = [P.alloc(BF16, (D_A,)) for _ in range(2)]
        gvb = P.alloc(F32, (D_A,))
        WT = P.alloc(BF16, (H_A, 128))
        wsf = Region(P.arena, "S", vtok.lo, F32, (H_A, 128))
        Wsamp = P.alloc(BF16, (H_A, 16))
        w00 = P.alloc(F32, (H_A,))
        b0col = P.alloc(F32, (H_A,))
        browf = Region(P.arena, "S", vtok.lo + 4096, F32, (H_A * 128,))
        brow = P.alloc(BF16, (H_A, 128))
        sqs = P.alloc(F32, (512,))
        st = P.alloc(F32, (8,))
        sqss = [sqs, P.alloc(F32, (512,))]
        sts = [P.alloc(F32, (8,)) for _ in range(2)]
        vs = P.alloc(F32, (D_A,))
        vsn = P.alloc(F32, (D_A,))
        P.dma("sp", wsf.full(), a_w_spatial[j].rearrange("g t s -> t g s"))
        P.dma("sp", gvb.full(), a_v_norm[j].partition_broadcast(128))
        with nc.allow_non_contiguous_dma(reason="tiny per-group scalars"):
            P.dma("sp", w00[0:16, :], a_w_spatial[j, :, 0, 0].partition_broadcast(16))
            P.dma("sp", b0col.full(), a_b_spatial[j, :, 0].partition_broadcast(128))
        P.dma("sp", browf[0:1, :], a_b_spatial[j].rearrange("g t -> (g t)").partition_broadcast(1))
        P.copy(brow[0:1, :, :], Region(P.arena, "S", browf.lo, F32, (H_A, 128))[0:1, :, :])
        P.op("pool", [wsf.full()], [wsf.full()],
             lambda g: g.affine_select(out=wsf.ap, in_=wsf.ap, pattern=[[0, H_A], [-1, 128]],
                                       compare_op=ALU.is_ge, fill=0.0, base=0, channel_multiplier=1))
        for half in range(2):
            ps = P.bank()
            for gg in range(4):
                P.transpose(ps[:, gg * 128:(gg + 1) * 128], wsf[:, half * 4 + gg, :], ident_f.full())
            P.copy(WT[:, half * 4:half * 4 + 4, :], Region(P.psum, "P", ps.lo, F32, (4, 128)).full())
        for g in range(H_A):
            P.ts(Wsamp[0:16, g, :], ident_f[0:16, 0:16], w00[0:16, g:g + 1], ALU.mult)
        win = a_w_in[j].rearrange("(k p) n -> p k n", p=128)
        wout = a_w_out[j].rearrange("(f p) n -> p f n", p=128)

        for tg in range(4):
            c0 = tg * 512
            subs = [(c0, 512, 0)] + ([(SEQ, NS, 512)] if tg == 3 else [])
            for (cc, n, o0) in subs:
                rmsnorm_tile(hTt, gmix, li, sq, rs, cc, n, o0)
            for fb in range(4):
                slot = WS[wctr[0] % 3]
                wctr[0] += 1
                P.dma("pool", slot.full(), win[:, :, fb * 512:(fb + 1) * 512])
                for fc in range(4):
                    for (cc, n, o0) in subs:
                        ps = P.bank()
                        P.mm(ps[:, 0:n], [(slot[:, k, fc * 128:(fc + 1) * 128], hTt[:, k, o0:o0 + n])
                                          for k in range(KC)])
                        P.act(uT[:, fb * 4 + fc, o0:o0 + n], ps[:, 0:n], AF.Gelu_apprx_tanh)
            for fb in range(4):
                slot = WS[wctr[0] % 3]
                wctr[0] += 1
                P.dma("pool", slot.full(), win[:, :, D_A + fb * 512:D_A + (fb + 1) * 512])
                for c in range(4):
                    ps = P.bank()
                    P.mm(ps.full(), [(hTt[:, k, c * 128:(c + 1) * 128], slot[:, k, :]) for k in range(KC)])
                    P.act(vtok[:, c, fb * 512:(fb + 1) * 512], ps.full(), AF.Gelu_apprx_tanh)
                if tg == 3:
                    ps = P.bank()
                    P.mm(ps[0:NS, :], [(hTt[:, k, 512:512 + NS], slot[:, k, :]) for k in range(KC)])
                    P.act(vs[0:NS, fb * 512:(fb + 1) * 512], ps[0:NS, :], AF.Gelu_apprx_tanh)
            def prep(c):
                stc = sts[c % 2]
                for fb in range(4):
                    sq_ = sqss[fb % 2]
                    P.act(sq_.full(), vtok[:, c, fb * 512:(fb + 1) * 512], AF.Square)
                    P.op("dve", [sq_.full()], [stc[:, fb:fb + 1]],
                         lambda e, fb=fb, sq_=sq_, stc=stc: e.reduce_sum(out=stc[:, fb:fb + 1].ap, in_=sq_.ap, axis=AX.X))
                P.op("dve", [stc[:, 0:4]], [stc[:, 4:5]],
                     lambda e, stc=stc: e.reduce_sum(out=stc[:, 4:5].ap, in_=stc[:, 0:4].ap, axis=AX.X))
                P.act(stc[:, 5:6], stc[:, 4:5], AF.Sqrt, bias=epsc[:, 0:1], scale=1.0 / D_A)
                P.recip(stc[:, 6:7], stc[:, 5:6])
                P.stt(vn[c % 2].full(), vtok[:, c, :], stc[:, 6:7], gvb.full(), ALU.mult, ALU.mult)

            def mixp(c):
                v_n = vn[c % 2]
                for q in range(4):
                    ps = P.bank()
                    for ff in range(4):
                        fc = q * 4 + ff
                        g = fc // 2
                        P.mm(ps[:, ff * 128:(ff + 1) * 128],
                             [(v_n[:, fc * 128:(fc + 1) * 128], WT[:, g, :]),
                              (one_row[0:1, 0:128], brow[0:1, g, :])])
                    P.tt(uT[:, q * 4:q * 4 + 4, c * 128:(c + 1) * 128],
                         Region(P.psum, "P", ps.lo, F32, (4, 128)).full(),
                         uT[:, q * 4:q * 4 + 4, c * 128:(c + 1) * 128], ALU.mult)

            prep(0)
            for c in range(4):
                if c + 1 < 4:
                    prep(c + 1)
                mixp(c)
            if tg == 3:
                for fb in range(4):
                    P.act(sqs[0:NS, :], vs[0:NS, fb * 512:(fb + 1) * 512], AF.Square)
                    P.op("dve", [sqs[0:NS, :]], [st[0:NS, fb:fb + 1]],
                         lambda e, fb=fb: e.reduce_sum(out=st[0:NS, fb:fb + 1].ap, in_=sqs[0:NS, :].ap, axis=AX.X))
                P.op("dve", [st[0:NS, 0:4]], [st[0:NS, 4:5]],
                     lambda e: e.reduce_sum(out=st[0:NS, 4:5].ap, in_=st[0:NS, 0:4].ap, axis=AX.X))
                P.act(st[0:NS, 5:6], st[0:NS, 4:5], AF.Sqrt, bias=epsc[0:NS, 0:1], scale=1.0 / D_A)
                P.recip(st[0:NS, 6:7], st[0:NS, 5:6])
                P.stt(vsn[0:NS, :], vs[0:NS, :], st[0:NS, 6:7], gvb[0:NS, :], ALU.mult, ALU.mult)
                P.dma("sp", ncv[j], vsn[0:NS, :])
                v_n = vn[0]
                P.copy(v_n[0:NS, :], vsn[0:NS, :])
                for q in range(4):
                    ps = P.bank()
                    for ff in range(4):
                        fc = q * 4 + ff
                        g = fc // 2
                        P.mm(ps[:, ff * 128:ff * 128 + NS], [(v_n[0:NS, fc * 128:(fc + 1) * 128], Wsamp[0:NS, g, :])])
                    for ff in range(4):
                        fc = q * 4 + ff
                        g = fc // 2
                        P.stt(uT[:, fc, 512:512 + NS], ps[:, ff * 128:ff * 128 + NS], b0col[:, g:g + 1],
                              uT[:, fc, 512:512 + NS], ALU.add, ALU.mult)
            for (f0, nf) in ((0, 4), (4, 4), (8, 4), (12, 4)):
                wo = OSLOT[octr[0] % 2]
                octr[0] += 1
                P.dma("pool", wo[:, 0:nf, :], wout[:, f0:f0 + nf, :])
                for (cc, n, o0) in subs:
                    for d in range(KC):
                        ps = P.bank()
                        P.mm(ps[:, 0:n], [(wo[:, jj, d * 128:(d + 1) * 128], uT[:, f0 + jj, o0:o0 + n])
                                          for jj in range(nf)])
                        P.tt(xT[:, d, cc:cc + n], xT[:, d, cc:cc + n], ps[:, 0:n], ALU.add)
        P.release(m)

    def R4(ps):
        return Region(P.psum, "P", ps.lo, F32, (4, 128))

    def mixer_b(li, l):
        m = P.mark()
        BT = 512
        NCH = BT // 128
        NTG = SEQ // BT
        NTT = BT + NS
        hTt = P.alloc(BF16, (KC, NTT))
        rs = P.alloc(F32, (1, BT))
        qn = P.alloc(BF16, (H_B, NTT))
        kn = P.alloc(BF16, (H_B, NTT))
        vT = P.alloc(BF16, (H_B, NTT))
        gateS = Region(P.arena, "S", qn.lo, BF16, (H_B, NTT))
        onT = Region(P.arena, "S", kn.lo, BF16, (H_B, NTT))
        oT = P.alloc(BF16, (H_B, NTT))
        sq = Region(P.arena, "S", oT.lo, BF16, (KC, BT))
        rbb = [P.alloc(BF16, (BT + 4,)) for _ in range(2)]
        dg = [P.alloc(BF16, (4, 128)) for _ in range(2)]
        acc = [P.alloc(BF16, (BT,)) for _ in range(2)]
        accS = [P.alloc(F32, (NS,)) for _ in range(2)]
        sqh = P.alloc(BF16, (BT,))
        halo = P.alloc(BF16, (24, 3))
        wconv = P.alloc(F32, (24, 4))
        wc4 = Region(P.arena, "S", qn.lo, F32, (QKV,))
        nA = P.alloc(F32, (H_B,))
        dtb = P.alloc(F32, (H_B,))
        gocol = P.alloc(F32, (1,))
        Utri = P.alloc(F32, (128,))
        Lmask = P.alloc(BF16, (128,))
        Amask = P.alloc(BF16, (128,))
        onesb4 = P.alloc(BF16, (128,))
        batok = P.alloc(F32, (NCH, 16))
        sc8s = [P.alloc(F32, (17, H_B)) for _ in range(2)]
        sc8 = sc8s[0]
        class CB:
            pass
        CBS = []
        for _ci in range(2):
            B = CB()
            B.base = P.top
            B.gbc = P.alloc(F32, (4, 128))
            B.dtmp = Region(P.arena, "S", B.gbc.lo, F32, (4, 128))
            B.Dm = P.alloc(BF16, (4, 128))
            B.Dms = P.alloc(BF16, (4, 128))
            B.egT = P.alloc(BF16, (4, 128))
            B.qg = P.alloc(BF16, (4, 128))
            B.L = P.alloc(F32, (4, 128))
            B.U = P.alloc(F32, (4, 128))
            B.Pm = P.alloc(F32, (4, 128))
            B.Am = P.alloc(BF16, (4, 128))
            B.TmT = Region(P.arena, "S", B.Am.lo, BF16, (4, 128))
            B.AT = Region(P.arena, "S", B.Dm.lo, BF16, (4, 128))
            B.kbg = P.alloc(BF16, (4, 128))
            B.kd = P.alloc(BF16, (4, 128))
            B.vb = P.alloc(BF16, (4, 128))
            B.utok = P.alloc(BF16, (4, 128))
            B.wT = Region(P.arena, "S", B.vb.lo, BF16, (4, 128))
            B.vnew = B.utok
            CBS.append(B)
        S_f = P.alloc(F32, (H_B, 128))
        S_b = P.alloc(BF16, (H_B, 128))
        rawS = Region(P.arena, "S", OSLOT[0].lo, F32, (24, NS))
        scT = Region(P.arena, "S", OSLOT[0].lo + 2048, F32, (24, 3, NS))
        assert 2048 + scT.nbytes <= OSLOT[0].nbytes
        sct = Region(P.arena, "S", CBS[1].base, F32, (3, 512))
        rowS = Region(P.arena, "S", CBS[0].base + 12288, F32, (512,))
        kq2 = P.alloc(F32, (H_B, NS, 2))
        bas = P.alloc(F32, (16,))
        decb = Region(P.arena, "S", CBS[0].base + 14336, F32, (NS, H_B))
        betab = Region(P.arena, "S", CBS[0].base + 14336 + 512, F32, (NS, H_B))
        kqb = Region(P.arena, "S", CBS[0].base + 14336 + 1024, F32, (H_B, NS))
        rhsd = Region(P.arena, "S", CBS[0].base + 14336 + 1536, F32, (NS, H_B))
        Sin = [Region(P.arena, "S", CBS[0].base + i * 4096, F32, (H_B, 128)) for i in range(2)]
        Snew = Region(P.arena, "S", CBS[0].base + 8192, F32, (H_B, 128))
        dSs = [P.alloc(F32, (6, H_B)) for _ in range(2)]
        Snews = [Snew, Region(P.arena, "S", CBS[1].base + 6144, F32, (H_B, 128))]
        dbc4 = Region(P.arena, "S", CBS[1].base + 10240, F32, (4, 128))
        dbcs = [dbc4, Region(P.arena, "S", CBS[1].base + 12288, F32, (4, 128))]
        Sin3 = Sin + [Region(P.arena, "S", CBS[1].base, F32, (H_B, 128))]
        assert CBS[1].base + 14336 <= S_f.lo
        P.dma("sp", wc4[0:4, :], b_w_conv[l])
        P.dma("sp", nA.full(), b_a_log[l].partition_broadcast(128))
        P.dma("sp", dtb.full(), b_dt_bias[l].partition_broadcast(128))
        with nc.allow_non_contiguous_dma(reason="128-element column"):
            P.dma("sp", gocol.full(), b_o_norm[l].rearrange("(p o) -> p o", o=1))
        ps = P.bank()
        for fc in range(24):
            P.transpose(ps[:, fc * 4:fc * 4 + 4], wc4[0:4, fc * 128:(fc + 1) * 128], ident_f[0:4, 0:4])
        P.copy(wconv.full(), Region(P.psum, "P", ps.lo, F32, (24, 4)).full())
        P.act(nA.full(), nA.full(), AF.Exp)
        P.ts(nA.full(), nA.full(), -1.0, ALU.mult)
        P.op("pool", [ones_f.full()], [Utri.full()],
             lambda g: g.affine_select(out=Utri.ap, in_=ones_f.ap, pattern=[[1, 128]], compare_op=ALU.is_ge,
                                       fill=0.0, base=0, channel_multiplier=-1))
        P.memset(onesb4.full(), 1.0)
        P.op("pool", [onesb4.full()], [Lmask.full()],
             lambda g: g.affine_select(out=Lmask.ap, in_=onesb4.ap, pattern=[[-1, 128]],
                                       compare_op=ALU.is_gt, fill=0.0, base=0, channel_multiplier=1))
        P.op("pool", [onesb4.full()], [Amask.full()],
             lambda g: g.affine_select(out=Amask.ap, in_=onesb4.ap, pattern=[[-1, 128]],
                                       compare_op=ALU.is_ge, fill=0.0, base=0, channel_multiplier=1))
        P.memset(halo.full(), 0.0)
        P.memset(S_f.full(), 0.0)
        P.memset(S_b.full(), 0.0)
        win = b_w_in[l].rearrange("(k p) n -> p k n", p=128)
        wout = b_w_out[l].rearrange("(f p) n -> p f n", p=128)
        rbc = [0]
        sqh2 = Region(P.arena, "S", rbb[0].lo, BF16, (BT,))
        rs2 = Region(P.arena, "S", dg[0].lo, F32, (1, BT))
        assert dg[1].lo == dg[0].lo + 1024 and rs2.nbytes <= 2048
        nrm = [(sqh, rs), (sqh2, rs2)]
        nrc = [0]

        def conv_silu(psr, fcg, n, dst):
            rb = rbb[rbc[0] % 2]
            d = dg[rbc[0] % 2]
            rbc[0] += 1
            P.copy(rb[:, 3:3 + n], psr)
            P.copy(rb[:, 0:3], halo[:, fcg, :])
            for jt in range(4):
                P.ts(d[:, jt, :], ident_b.full(), wconv[:, fcg, jt:jt + 1], ALU.mult)
            ps2 = P.bank()
            P.mm(ps2[:, 0:n], [(d[:, jt, :], rb[:, jt:jt + n]) for jt in range(4)])
            P.copy(halo[:, fcg, :], rb[:, n:n + 3])
            P.act(dst, ps2[:, 0:n], AF.Silu)

        def rsqrt_act(out, in_, np_=128):
            P.act(out, in_, AF.Ln, bias=epsc[0:np_, 0:1])
            P.act(out, out, AF.Exp, scale=-0.5)

        def l2n(src, dst, n, scale):
            sq_, rs_ = nrm[nrc[0] % 2]
            nrc[0] += 1
            P.act(sq_[:, 0:n], src, AF.Square)
            ps = P.bank()
            P.mm(ps[:, 0:n], [(one_row.full(), sq_[:, 0:n])])
            rsqrt_act(rs_[:, 0, 0:n], ps[:, 0:n])
            P.stt(dst, src, scale, rs_[:, 0, 0:n], ALU.mult, ALU.mult)

        def gate_decay(sc, np_, a_raw, gout):
            r = lambda i: sc[0:np_, i, :]
            P.tt(r(10), a_raw, dtb[0:np_, :], ALU.add)
            P.act(r(11), r(10), AF.Exp)
            P.act(r(12), r(11), AF.Ln, bias=1.0)
            P.ts(r(13), r(11), 2.0, ALU.add)
            P.recip(r(13), r(13))
            P.tt(r(13), r(13), r(11), ALU.mult)
            P.tt(r(14), r(13), r(13), ALU.mult)
            P.ts(r(15), r(14), 1.0 / 9, ALU.mult, 1.0 / 7, ALU.add)
            P.tt(r(15), r(15), r(14), ALU.mult)
            P.ts(r(15), r(15), 1.0 / 5, ALU.add)
            P.tt(r(15), r(15), r(14), ALU.mult)
            P.ts(r(15), r(15), 1.0 / 3, ALU.add)
            P.tt(r(15), r(15), r(14), ALU.mult)
            P.ts(r(15), r(15), 1.0, ALU.add)
            P.tt(r(15), r(15), r(13), ALU.mult)
            P.ts(r(15), r(15), 2.0, ALU.mult)
            P.ts(r(16), r(11), 1.0, ALU.is_le)
            P.tt(r(15), r(15), r(12), ALU.subtract)
            P.tt(r(15), r(15), r(16), ALU.mult)
            P.tt(r(15), r(15), r(12), ALU.add)
            P.tt(gout, r(15), nA[0:np_, :], ALU.mult)

        def sample_delta():
            with nc.allow_non_contiguous_dma(reason="hbm->hbm state row copy"):
                P.dma("sp", ncs[l][:, 0:2, :], sc[l][:, 1:3, :])
            P.dma("sp", Sin[0].full(), sd[l, 0].rearrange("h k v -> k h v"))
            for fcg in range(24):
                a = accS[fcg % 2]
                P.ts(a[:, 0:NS], scT[:, fcg, 0, :], wconv[:, fcg, 0:1], ALU.mult)
                P.stt(a[:, 0:NS], scT[:, fcg, 1, :], wconv[:, fcg, 1:2], a[:, 0:NS], ALU.mult, ALU.add)
                P.stt(a[:, 0:NS], scT[:, fcg, 2, :], wconv[:, fcg, 2:3], a[:, 0:NS], ALU.mult, ALU.add)
                P.stt(a[:, 0:NS], rawS[:, fcg, :], wconv[:, fcg, 3:4], a[:, 0:NS], ALU.mult, ALU.add)
                P.act(rawS[:, fcg, :], a[:, 0:NS], AF.Silu)
            for fcg in range(16):
                h = fcg % 8
                l2n(rawS[:, fcg, :], kq2[:, h, :, 1 if fcg < 8 else 0], NS, 128.0 ** -0.5 if fcg < 8 else 1.0)
            r = lambda i: sc8[0:NS, i, :]
            P.act(r(0), bas[0:NS, 0:8], AF.Exp, scale=-1.0)
            P.ts(r(0), r(0), 1.0, ALU.add)
            P.recip(r(0), r(0))
            gate_decay(sc8, NS, bas[0:NS, 8:16], r(2))
            P.act(r(3), r(2), AF.Exp)
            for (src, dst) in ((r(3), decb), (r(0), betab)):
                for t in range(NS):
                    P.ts(rhsd[0:NS, t, :], src, ident_f[0:NS, t:t + 1], ALU.mult)
                ps = P.bank()
                P.mm(ps[:, 0:NS * H_B], [(ones_f[0:NS, 0:128], Region(P.arena, "S", rhsd.lo, F32, (NS * H_B,))[0:NS, :])])
                P.copy(dst.full(), Region(P.psum, "P", ps.lo, F32, (NS, H_B)).full())
            P.tt(kqb.full(), kq2[:, :, :, 0], kq2[:, :, :, 1], ALU.mult)
            ps = P.bank()
            P.mm(ps[:, 0:H_B * NS], [(ones_f.full(), Region(P.arena, "S", kqb.lo, F32, (H_B * NS,)).full())])
            P.copy(kqb.full(), Region(P.psum, "P", ps.lo, F32, (H_B, NS)).full())
            def tok(t):
                Si = Sin3[t % 3]
                Sn = Snews[t % 2]
                dS = dSs[t % 2]
                db = dbcs[t % 2]
                if t + 1 < NS:
                    P.dma("sp", Sin3[(t + 1) % 3].full(), sd[l, t + 1].rearrange("h k v -> k h v"))
                ps = P.bank()
                for h in range(H_B):
                    P.mm(ps[:, h * 2:h * 2 + 2], [(Si[:, h, :], kq2[:, h, t, :])])
                yield
                pv2 = Region(P.psum, "P", ps.lo, F32, (H_B, 2))
                d_ = lambda i: dS[:, i, :]
                P.tt(d_(0), pv2[:, :, 0], decb[:, t, :], ALU.mult)
                P.tt(d_(0), rawS[:, 16:24, t], d_(0), ALU.subtract)
                P.tt(d_(0), d_(0), betab[:, t, :], ALU.mult)
                P.tt(d_(1), pv2[:, :, 1], decb[:, t, :], ALU.mult)
                P.tt(d_(2), d_(0), kqb[:, :, t], ALU.mult)
                P.tt(oT[:, :, BT + t], d_(1), d_(2), ALU.add)
                for half in range(2):
                    pb = P.bank()
                    dv = dS[:, 0, half * 4:half * 4 + 4]
                    P.tt(db.full(), bcm4(ident_f),
                         V(dv.ap.unsqueeze(2).broadcast_to([128, 4, 128]), dv.space, dv.lo, dv.hi), ALU.mult)
                    P.mm(pb.full(), [(ones_f.full(), Region(P.arena, "S", db.lo, F32, (512,)).full())])
                    yield
                    for i in range(4):
                        h = half * 4 + i
                        P.act(Sn[:, h, :], Si[:, h, :], AF.Identity, scale=decb[:, t, h:h + 1])
                        P.stt(Sn[:, h, :], R4(pb)[:, i, :], kq2[:, h, t, 0:1], Sn[:, h, :], ALU.mult, ALU.add)
                P.dma("sp", nds[l, t].rearrange("h k v -> k h v"), Sn.full())

            todo = list(range(NS))
            active = []
            while todo or active:
                while todo and len(active) < 2:
                    active.append(tok(todo.pop(0)))
                for g in list(active):
                    try:
                        next(g)
                    except StopIteration:
                        active.remove(g)

        def chain(cs, hh, B, sc8):
            hs = [hh * 4 + i for i in range(4)]
            P.tt(B.gbc.full(), bcm4(Utri), bc4(sc8, 2, hs[0]), ALU.mult)
            psg = P.bank()
            P.mm(psg.full(), [(ones_f.full(), Region(P.arena, "S", B.gbc.lo, F32, (512,)).full())])
            yield
            P.tt(B.dtmp.full(), R4(psg).full(), bc4(sc8, 3, hs[0]), ALU.subtract)
            P.ts(B.dtmp.full(), B.dtmp.full(), 0.0, ALU.max)
            P.act(B.egT.full(), R4(psg).full(), AF.Exp)
            P.act(B.Dm.full(), B.dtmp.full(), AF.Exp, scale=-1.0)
            P.tt(B.Dms.full(), B.Dm.full(), bcm4(Lmask), ALU.mult)
            P.tt(B.Dms.full(), B.Dms.full(), bc4(sc8, 0, hs[0]), ALU.mult)
            P.tt(B.Dm.full(), B.Dm.full(), bcm4(Amask), ALU.mult)
            P.tt(B.qg.full(), qn[:, hs[0]:hs[0] + 4, cs], B.egT.full(), ALU.mult)
            pkk = P.bank()
            for i, h in enumerate(hs):
                P.mm(R4(pkk)[:, i, :], [(kn[:, h, cs], kn[:, h, cs])])
            pqk = P.bank()
            for i, h in enumerate(hs):
                P.mm(R4(pqk)[:, i, :], [(qn[:, h, cs], kn[:, h, cs])])
            yield
            P.tt(B.L.full(), R4(pkk).full(), B.Dms.full(), ALU.mult)
            P.tt(B.Am.full(), R4(pqk).full(), B.Dm.full(), ALU.mult)
            pu = P.bank()
            for i in range(4):
                P.transpose_hw(R4(pu)[:, i, :], B.L[:, i, :], ident_f.full())
            pa = P.bank()
            for i in range(4):
                P.transpose(R4(pa)[:, i, :], B.Am[:, i, :], ident_b.full())
            yield
            P.copy(B.U.full(), R4(pu).full(), eng="act")
            P.tt(B.Pm.full(), bcm4(ident_b), R4(pu).full(), ALU.subtract)
            P.copy(B.AT.full(), R4(pa).full(), eng="act")
            for stp in range(6):
                p1 = P.bank()
                for i in range(4):
                    P.mm(R4(p1)[:, i, :], [(B.U[:, i, :], B.L[:, i, :])])
                yield
                P.copy(B.L.full(), R4(p1).full(), eng="act")
                p3 = P.bank()
                for i in range(4):
                    P.mm(R4(p3)[:, i, :], [(B.L[:, i, :], B.Pm[:, i, :])])
                if stp < 5:
                    p2 = P.bank()
                    for i in range(4):
                        P.transpose_hw(R4(p2)[:, i, :], B.L[:, i, :], ident_f.full())
                yield
                P.tt(B.Pm.full(), B.Pm.full(), R4(p3).full(), ALU.add)
                if stp < 5:
                    P.copy(B.U.full(), R4(p2).full())
            P.copy(B.TmT.full(), B.Pm.full(), eng="act")
            pk = P.bank()
            for i, h in enumerate(hs):
                P.transpose(R4(pk)[:, i, :], kn[:, h, cs], ident_b.full())
            pv = P.bank()
            for i, h in enumerate(hs):
                P.transpose(R4(pv)[:, i, :], vT[:, h, cs], ident_b.full())
            yield
            P.tt(B.kbg.full(), R4(pk).full(), bc4(sc8, 7, hs[0]), ALU.mult)
            P.tt(B.kd.full(), R4(pk).full(), bc4(sc8, 6, hs[0]), ALU.mult)
            P.tt(B.vb.full(), R4(pv).full(), bc4(sc8, 0, hs[0]), ALU.mult)
            pu2 = P.bank()
            for i in range(4):
                P.mm(R4(pu2)[:, i, :], [(B.TmT[:, i, :], B.vb[:, i, :])])
            pw = P.bank()
            for i in range(4):
                P.mm(R4(pw)[:, i, :], [(B.kbg[:, i, :], B.TmT[:, i, :])])
            yield
            P.copy(B.utok.full(), R4(pu2).full(), eng="act")
            P.copy(B.wT.full(), R4(pw).full())
            pws = P.bank()
            for i, h in enumerate(hs):
                P.mm(R4(pws)[:, i, :], [(B.wT[:, i, :], S_b[:, h, :])])
            yield
            P.tt(B.vnew.full(), B.utok.full(), R4(pws).full(), ALU.subtract)
            po = P.bank()
            for i, h in enumerate(hs):
                P.mm(R4(po)[:, i, :], [(S_b[:, h, :], B.qg[:, i, :]), (B.vnew[:, i, :], B.AT[:, i, :])])
            pS = P.bank()
            for i, h in enumerate(hs):
                P.mm(R4(pS)[:, i, :], [(B.kd[:, i, :], B.vnew[:, i, :])])
            yield
            P.copy(oT[:, hs[0]:hs[0] + 4, cs], R4(po).full(), eng="act")
            P.tt(S_f[:, hs[0]:hs[0] + 4, :], S_f[:, hs[0]:hs[0] + 4, :], bc4(sc8, 5, hs[0]), ALU.mult)
            P.tt(S_f[:, hs[0]:hs[0] + 4, :], S_f[:, hs[0]:hs[0] + 4, :], R4(pS).full(), ALU.add)
            P.copy(S_b[:, hs[0]:hs[0] + 4, :], S_f[:, hs[0]:hs[0] + 4, :], eng="act")

        for tg in range(NTG):
            c0 = tg * BT
            last = (tg == NTG - 1)
            rmsnorm_tile(hTt, gmix, li, sq, rs, c0, BT, 0)
            if last:
                rmsnorm_tile(hTt, gmix, li, sq, rs, SEQ, NS, BT)
            pend = [None]
            for fb in range(6):
                slot = WSLOT[wctr[0] % NW]
                wctr[0] += 1
                P.dma("pool", slot.full(), win[:, :, fb * 512:(fb + 1) * 512])
                for fc in range(4):
                    fcg = fb * 4 + fc
                    ps = P.bank()
                    P.mm(ps[:, 0:BT], [(slot[:, k, fc * 128:(fc + 1) * 128], hTt[:, k, 0:BT]) for k in range(KC)])
                    h = fcg % 8
                    if pend[0] is not None:
                        pend[0]()
                    pend[0] = (lambda ps=ps, fcg=fcg, h=h, fb=fb:
                               conv_silu(ps[:, 0:BT], fcg, BT, (qn if fb < 2 else kn if fb < 4 else vT)[:, h, 0:BT]))
                if fb == 5:
                    pend[0]()
                    pend[0] = None
                if last:
                    ps = P.bank()
                    P.mm(ps[0:3, :], [(hTt[:, k, BT - 3:BT], slot[:, k, :]) for k in range(KC)])
                    P.copy(rowS[0:3, :], ps[0:3, :])
                    P.dma("sp", ncp[l][:, fb * 512:(fb + 1) * 512], rowS[0:3, :])
                    ps = P.bank()
                    for fc in range(4):
                        P.mm(ps[:, fc * NS:(fc + 1) * NS],
                             [(slot[:, k, fc * 128:(fc + 1) * 128], hTt[:, k, BT:BT + NS]) for k in range(KC)])
                    P.copy(rawS[:, fb * 4:fb * 4 + 4, :], Region(P.psum, "P", ps.lo, F32, (4, NS)).full())
                    ps = P.bank()
                    P.mm(ps[0:NS, :], [(hTt[:, k, BT:BT + NS], slot[:, k, :]) for k in range(KC)])
                    P.copy(rowS[0:NS, :], ps[0:NS, :])
                    P.dma("sp", ncs[l][:, 2, fb * 512:(fb + 1) * 512], rowS[0:NS, :])
                    P.dma("sp", sct[0:NS, :, :], sc[l][:, :, fb * 512:(fb + 1) * 512])
                    ps = P.bank()
                    for fc in range(4):
                        for jt in range(3):
                            P.transpose(ps[:, (fc * 3 + jt) * NS:(fc * 3 + jt + 1) * NS],
                                        sct[0:NS, jt, fc * 128:(fc + 1) * 128], ident_f[0:NS, 0:NS])
                    P.copy(scT[:, fb * 4:fb * 4 + 4, :, :], Region(P.psum, "P", ps.lo, F32, (4, 3, NS)).full())
            for (buf, scl) in ((qn, 128.0 ** -0.5), (kn, 1.0)):
                P.act(sq[:, :, 0:BT], buf[:, :, 0:BT], AF.Square)
                for h in range(H_B):
                    ps = P.bank()
                    P.mm(ps[:, 0:BT], [(one_row.full(), sq[:, h, 0:BT])])
                    sq_, rs_ = nrm[nrc[0] % 2]
                    nrc[0] += 1
                    rsqrt_act(rs_[:, 0, 0:BT], ps[:, 0:BT])
                    P.stt(buf[:, h, 0:BT], buf[:, h, 0:BT], scl, rs_[:, 0, 0:BT], ALU.mult, ALU.mult)
            slot = WSLOT[wctr[0] % NW]
            wctr[0] += 1
            P.dma("pool", slot[:, :, 0:16], win[:, :, 4096:4112])
            for c in range(NCH):
                ps = P.bank()
                P.mm(ps[:, 0:16], [(hTt[:, k, c * 128:(c + 1) * 128], slot[:, k, 0:16]) for k in range(KC)])
                P.copy(batok[:, c, :], ps[:, 0:16])
            if last:
                ps = P.bank()
                P.mm(ps[0:NS, 0:16], [(hTt[:, k, BT:BT + NS], slot[:, k, 0:16]) for k in range(KC)])
                P.copy(bas[0:NS, :], ps[0:NS, 0:16])
            for c in range(NCH):
                cs = slice(c * 128, (c + 1) * 128)
                sc8 = sc8s[c % 2]
                beta = sc8[:, 0, :]
                P.act(beta, batok[:, c, 0:8], AF.Exp, scale=-1.0)
                P.ts(beta, beta, 1.0, ALU.add)
                P.recip(beta, beta)
                gtok = sc8[:, 2, :]
                gate_decay(sc8, 128, batok[:, c, 8:16], gtok)
                ps = P.bank()
                P.mm(ps[:, 0:8], [(Utri.full(), gtok)])
                P.mm(ps[:, 8:16], [(ones_f.full(), gtok)])
                gct = sc8[:, 3, :]
                P.copy(gct, ps[:, 0:8])
                P.act(sc8[:, 4, :], ps[:, 0:8], AF.Exp)
                P.act(sc8[:, 5, :], ps[:, 8:16], AF.Exp)
                P.tt(sc8[:, 6, :], ps[:, 8:16], gct, ALU.subtract)
                P.act(sc8[:, 6, :], sc8[:, 6, :], AF.Exp)
                P.tt(sc8[:, 7, :], beta, sc8[:, 4, :], ALU.mult)
                gens = [chain(cs, hh, CBS[hh], sc8) for hh in range(2)]
                while gens:
                    for g in list(gens):
                        try:
                            next(g)
                        except StopIteration:
                            gens.remove(g)
            if last:
                P.dma("sp", ndp[l].rearrange("h k v -> k h v"), S_f.full())
                sample_delta()
            n = BT
            for fb in (6, 7):
                slot = WSLOT[wctr[0] % NW]
                wctr[0] += 1
                P.dma("pool", slot.full(), win[:, :, fb * 512:(fb + 1) * 512])
                for fc in range(4):
                    ps = P.bank()
                    P.mm(ps[:, 0:BT], [(slot[:, k, fc * 128:(fc + 1) * 128], hTt[:, k, 0:BT]) for k in range(KC)])
                    P.act(gateS[:, (fb - 6) * 4 + fc, 0:BT], ps[:, 0:BT], AF.Silu)
                    if last:
                        ps = P.bank()
                        P.mm(ps[:, 0:NS], [(slot[:, k, fc * 128:(fc + 1) * 128], hTt[:, k, BT:BT + NS])
                                           for k in range(KC)])
                        P.act(gateS[:, (fb - 6) * 4 + fc, BT:BT + NS], ps[:, 0:NS], AF.Silu)
            subs = [(c0, BT, 0)] + ([(SEQ, NS, BT)] if last else [])
            for h in range(H_B):
                for (cc, n, o0) in subs:
                    sq_, rs_ = nrm[nrc[0] % 2]
                    ac_ = acc[nrc[0] % 2]
                    nrc[0] += 1
                    P.act(sq_[:, 0:n], oT[:, h, o0:o0 + n], AF.Square)
                    ps = P.bank()
                    P.mm(ps[:, 0:n], [(ones128_b.full(), sq_[:, 0:n])])
                    rsqrt_act(rs_[:, 0, 0:n], ps[:, 0:n])
                    P.stt(ac_[:, 0:n], oT[:, h, o0:o0 + n], gocol[:, 0:1], rs_[:, 0, 0:n], ALU.mult, ALU.mult)
                    P.tt(onT[:, h, o0:o0 + n], ac_[:, 0:n], gateS[:, h, o0:o0 + n], ALU.mult)
            for (f0, nf) in ((0, 4), (4, 4)):
                wo = OSLOT[octr[0] % 2]
                octr[0] += 1
                P.dma("pool", wo[:, 0:nf, :], wout[:, f0:f0 + nf, :])
                for (cc, n, o0) in subs:
                    for d in range(KC):
                        ps = P.bank()
                        P.mm(ps[:, 0:n], [(wo[:, jj, d * 128:(d + 1) * 128], onT[:, f0 + jj, o0:o0 + n])
                                          for jj in range(nf)])
                        P.tt(xT[:, d, cc:cc + n], xT[:, d, cc:cc + n], ps[:, 0:n], ALU.add)
        P.release(m)

    def gtok_col(sc8, row, h):
        return sc8[:, row, h:h + 1]

    def bc4(sc8, row, h0):
        v = sc8[:, row, h0:h0 + 4]
        return V(v.ap.unsqueeze(2).broadcast_to([128, 4, 128]), v.space, v.lo, v.hi)

    def bcm4(r):
        v = r.full()
        return V(v.ap.unsqueeze(1).broadcast_to([128, 4, 128]), v.space, v.lo, v.hi)

    def ones4():
        v = ones_f.full()
        return V(v.ap.unsqueeze(1).broadcast_to([128, 4, 128]), v.space, v.lo, v.hi)

    if not cfg.get("skip_load"):
        load_x()
    li_of = {"A": 0, "B": 0, "F": 0}
    lidx = 0
    for kind in layers:
        if kind == "F":
            ffn(cfg.get("ffn_li", [0, 1, 2, 3])[li_of["F"]])
            li_of["F"] += 1
        elif kind == "B":
            mixer_b(2 * li_of["B"] + 1, li_of["B"])
            li_of["B"] += 1
        elif kind == "A":
            mixer_a(2 * li_of["A"], li_of["A"])
            li_of["A"] += 1
    if not cfg.get("skip_final"):
        final_out()
    P.finish()
    cfg["stats"] = dict(nops=P.nops, nwaits=P.nwaits, cnt=dict(P.cnt), dn=dict(P.dn), top=P.top)


def kernel(**inputs):
    cfg = {}
    nc = build_program(cfg)
    in_maps = []
    f = lambda a: np.ascontiguousarray(np.asarray(a, dtype=np.float32))
    shared = {k: f(inputs[k]) for k in (
        "norm_mix", "norm_ffn", "norm_final", "a_w_in", "a_v_norm", "a_w_spatial", "a_b_spatial", "a_w_out",
        "b_w_in", "b_w_conv", "b_a_log", "b_dt_bias", "b_o_norm", "b_w_out", "ffn_w_in", "ffn_w_out")}
    x_prompt = f(inputs["x_prompt"])
    x_sample = f(inputs["x_sample"])
    state_delta = f(inputs["state_delta"])
    state_conv = f(inputs["state_conv"])
    for c in range(NCORES):
        m = dict(shared)
        m["xp"] = np.ascontiguousarray(x_prompt[c])
        m["xs"] = np.ascontiguousarray(x_sample[c * NS:(c + 1) * NS, 0, :])
        m["sd"] = np.ascontiguousarray(state_delta[:, c * NS:(c + 1) * NS])
        m["sc"] = np.ascontiguousarray(state_conv[:, c * NS:(c + 1) * NS])
        in_maps.append(m)
    res = run_bass_kernel_spmd(nc, in_maps, core_ids=list(range(NCORES)))
    R = res.results
    y_prompt = np.stack([R[c]["yp"] for c in range(NCORES)], axis=0)
    y_sample = np.concatenate([R[c]["ys"] for c in range(NCORES)], axis=0)[:, None, :]
    ndp = np.stack([R[c]["ndp"] for c in range(NCORES)], axis=1)
    ncp = np.stack([R[c]["ncp"] for c in range(NCORES)], axis=1)
    nds = np.concatenate([R[c]["nds"] for c in range(NCORES)], axis=1)
    ncs = np.concatenate([R[c]["ncs"] for c in range(NCORES)], axis=1)
    ncv = np.concatenate([R[c]["ncv"] for c in range(NCORES)], axis=1)[:, :, None, :]
    return (y_prompt.astype(np.float32), y_sample.astype(np.float32), ndp.astype(np.float32),
            ncp.astype(np.float32), nds.astype(np.float32), ncs.astype(np.float32), ncv.astype(np.float32))
```

```python
import contextlib
import numpy as np
import concourse.bass as bass
import concourse.mybir as mybir
from concourse.bass_utils import run_bass_kernel_spmd

F32 = mybir.dt.float32
BF16 = mybir.dt.bfloat16
AF = mybir.ActivationFunctionType
ALU = mybir.AluOpType
AX = mybir.AxisListType

NCORES = 8
D = 1024
KC = 8
SEQ = 2048
NS = 16
NT = SEQ + NS
DEPTH = 4
D_A = 2048
H_A = 8
D_FF = 2816
H_B = 8
QKV = 3072
B_IN = 4112
EPS = 1e-6
TILES = [(0, 512), (512, 512), (1024, 512), (1536, 512), (2048, NS)]
ESZ = {F32: 4, BF16: 2}


class V:
    __slots__ = ("ap", "space", "lo", "hi")

    def __init__(self, ap, space, lo, hi):
        self.ap, self.space, self.lo, self.hi = ap, space, lo, hi


class Region:
    def __init__(self, base, space, byte_lo, dtype, shape, parts=128):
        self.space, self.lo, self.dtype, self.shape, self.parts = space, byte_lo, dtype, tuple(shape), parts
        es = ESZ[dtype]
        n = int(np.prod(shape))
        self.nbytes = n * es
        assert byte_lo % 4 == 0
        ap = base[0:parts, byte_lo // 4:(byte_lo + self.nbytes + 3) // 4]
        if dtype != F32:
            ap = ap.bitcast(dtype)
        if len(shape) > 1:
            names = " ".join("d%d" % i for i in range(len(shape)))
            kw = {"d%d" % i: shape[i] for i in range(1, len(shape))}
            ap = ap.rearrange("p (%s) -> p %s" % (names, names), **kw)
        self.ap = ap
        st = [1] * len(shape)
        for i in range(len(shape) - 2, -1, -1):
            st[i] = st[i + 1] * shape[i + 1]
        self.strides = st

    def __getitem__(self, idx):
        if not isinstance(idx, tuple):
            idx = (idx,)
        idx = idx + (slice(None),) * (1 + len(self.shape) - len(idx))
        ap = self.ap[idx]
        es = ESZ[self.dtype]
        lo = 0
        hi = 0
        for i, ix in enumerate(idx[1:]):
            if isinstance(ix, slice):
                a = 0 if ix.start is None else ix.start
                b = self.shape[i] if ix.stop is None else ix.stop
                assert ix.step in (None, 1) and 0 <= a < b <= self.shape[i], (ix, self.shape)
            else:
                a, b = ix, ix + 1
                assert 0 <= a < self.shape[i]
            lo += a * self.strides[i]
            hi += (b - 1) * self.strides[i]
        blo, bhi = self.lo + lo * es, self.lo + (hi + 1) * es
        if self.space == "P":
            blo = blo // 2048 * 2048
            bhi = (bhi + 2047) // 2048 * 2048
        return V(ap, self.space, blo, bhi)

    def full(self):
        return self[(slice(None),)]


class Prog:
    def __init__(self, nc, es, arena_bytes=207 * 1024):
        self.nc = nc
        self.es = es
        self.arena = es.enter_context(nc.sbuf_tensor("arena", [128, arena_bytes // 4], F32))
        self.psum = es.enter_context(nc.psum_tensor("psum", [128, 4096], F32))
        self.arena_bytes = arena_bytes
        self.top = 0
        self.eng = {"pe": nc.tensor, "act": nc.scalar, "dve": nc.vector, "pool": nc.gpsimd, "sp": nc.sync}
        self.sem = {e: es.enter_context(nc.semaphore("c_" + e)) for e in self.eng}
        self.cnt = {e: 0 for e in self.eng}
        self.NDS = 8
        self.dsem = {q: [es.enter_context(nc.semaphore("d_%s%d" % (q, i))) for i in range(self.NDS)]
                     for q in ("sp", "pool")}
        self.dn = {q: 0 for q in ("sp", "pool")}
        self.waited = {e: {} for e in self.eng}
        self.recs = {"S": [], "P": []}
        self.bank_rr = 0
        self.nwaits = 0
        self.nops = 0

    def alloc(self, dtype, shape, parts=128):
        n = int(np.prod(shape)) * ESZ[dtype]
        n = (n + 31) // 32 * 32
        lo = self.top
        self.top += n
        assert self.top <= self.arena_bytes, ("SBUF arena overflow", self.top)
        return Region(self.arena, "S", lo, dtype, shape, parts)

    def mark(self):
        return self.top

    def release(self, m):
        self.top = m

    def bank(self, dtype=F32, shape=None, parts=128, b=None):
        if b is None:
            b = self.bank_rr
            self.bank_rr = (self.bank_rr + 1) % 8
        if shape is None:
            shape = (2048 // ESZ[dtype],)
        return Region(self.psum, "P", b * 2048, dtype, shape, parts)

    def bank2(self, dtype=F32, shape=None, parts=128):
        if self.bank_rr % 2:
            self.bank_rr = (self.bank_rr + 1) % 8
        b = self.bank_rr
        self.bank_rr = (self.bank_rr + 2) % 8
        if shape is None:
            shape = (4096 // ESZ[dtype],)
        return Region(self.psum, "P", b * 2048, dtype, shape, parts)

    def _collect(self, reads, writes, me=None):
        deps = {}
        for lst, isw in ((reads, False), (writes, True)):
            for v in lst:
                isp = v.space == "P"
                for r in self.recs[v.space]:
                    if r[0] < v.hi and v.lo < r[1] and (isw or r[4] or (isp and r[2] != me)):
                        k = r[2]
                        if deps.get(k, (None, 0))[1] < r[3]:
                            deps[k] = (r[5], r[3])
        return deps

    def _record(self, reads, writes, semkey, sem, val):
        for v in writes:
            rl = self.recs[v.space]
            rl[:] = [r for r in rl if not (v.lo <= r[0] and r[1] <= v.hi)]
            rl.append([v.lo, v.hi, semkey, val, True, sem])
        for v in reads:
            rl = self.recs[v.space]
            for r in rl:
                if r[0] == v.lo and r[1] == v.hi and r[2] == semkey and not r[4]:
                    r[3] = val
                    break
            else:
                rl.append([v.lo, v.hi, semkey, val, False, sem])

    def _waits(self, e, deps, skip_self=False):
        w = self.waited[e]
        for k, (sem, val) in deps.items():
            if skip_self and k == e:
                continue
            if w.get(k, 0) < val:
                self.eng[e].wait_ge(sem, val)
                w[k] = val
                self.nwaits += 1

    def op(self, e, reads, writes, fn):
        deps = self._collect(reads, writes, e)
        self._waits(e, deps, skip_self=(e == "pe"))
        ins = fn(self.eng[e])
        self.cnt[e] += 1
        ins.then_inc(self.sem[e], 1)
        self._record(reads, writes, e, self.sem[e], self.cnt[e])
        self.nops += 1

    def dma(self, q, out, in_, out_v=None, in_v=None):
        reads, writes = [], []
        if isinstance(in_, V):
            reads.append(in_)
            in_ = in_.ap
        if isinstance(out, V):
            writes.append(out)
            out = out.ap
        n = self.dn[q]
        s = self.dsem[q][n % self.NDS]
        tgt = 16 * (n // self.NDS + 1)
        self.dn[q] += 1
        key = "%s_d%d" % (q, n % self.NDS)
        deps = self._collect(reads, writes)
        if tgt > 16:
            deps[key] = (s, max(deps.get(key, (None, 0))[1], tgt - 16))
        self._waits(q, deps)
        self.eng[q].dma_start(out=out, in_=in_).then_inc(s, 16)
        self._record(reads, writes, key, s, tgt)

    def finish(self):
        for q in ("sp", "pool"):
            for i in range(self.NDS):
                n = self.dn[q]
                cnt = n // self.NDS + (1 if i < n % self.NDS else 0)
                if cnt:
                    self.nc.sync.wait_ge(self.dsem[q][i], 16 * cnt)

    def mm(self, out, pairs, extra_reads=()):
        reads = list(extra_reads)
        for l, r in pairs:
            reads += [l, r]
        n = len(pairs)

        def fn(pe):
            ins = None
            for i, (l, r) in enumerate(pairs):
                ins = pe.matmul(out.ap, l.ap, r.ap, start=(i == 0), stop=(i == n - 1))
            return ins
        self.op("pe", reads, [out], fn)

    def transpose(self, out, in_, ident):
        self.op("pe", [in_, ident], [out], lambda pe: pe.matmul(out.ap, in_.ap, ident.ap, start=True, stop=True))

    def transpose_hw(self, out, in_, ident):
        self.op("pe", [in_, ident], [out], lambda pe: pe.transpose(out.ap, in_.ap, ident.ap))

    def act(self, out, in_, func, bias=None, scale=1.0, accum=None, eng="act"):
        reads = [in_]
        kw = {}
        if bias is not None:
            if isinstance(bias, V):
                reads.append(bias)
                kw["bias"] = bias.ap
            else:
                kw["bias"] = bias
        if isinstance(scale, V):
            reads.append(scale)
            kw["scale"] = scale.ap
        else:
            kw["scale"] = scale
        writes = [out]
        if accum is not None:
            writes.append(accum)
            kw["accum_out"] = accum.ap
        self.op("act", reads, writes, lambda a: a.activation(out=out.ap, in_=in_.ap, func=func, **kw))

    def tt(self, out, a, b, op, eng="dve"):
        self.op(eng, [a, b], [out], lambda e: e.tensor_tensor(out=out.ap, in0=a.ap, in1=b.ap, op=op))

    def ts(self, out, a, s1, op0, s2=None, op1=None, eng="dve"):
        reads = [a]
        s1a, s2a = s1, s2
        if isinstance(s1, V):
            reads.append(s1)
            s1a = s1.ap
        if isinstance(s2, V):
            reads.append(s2)
            s2a = s2.ap
        kw = {}
        if op1 is not None:
            kw["op1"] = op1
        self.op(eng, reads, [out],
                lambda e: e.tensor_scalar(out=out.ap, in0=a.ap, scalar1=s1a, scalar2=s2a, op0=op0, **kw))

    def stt(self, out, a, s, b, op0, op1, eng="dve"):
        reads = [a, b]
        sa = s
        if isinstance(s, V):
            reads.append(s)
            sa = s.ap
        self.op(eng, reads, [out],
                lambda e: e.scalar_tensor_tensor(out=out.ap, in0=a.ap, scalar=sa, in1=b.ap, op0=op0, op1=op1))

    def copy(self, out, in_, eng="dve"):
        if eng == "act":
            self.op("act", [in_], [out], lambda a: a.copy(out=out.ap, in_=in_.ap))
        else:
            self.op(eng, [in_], [out], lambda e: e.tensor_copy(out=out.ap, in_=in_.ap))

    def memset(self, out, val, eng="dve"):
        self.op(eng, [], [out], lambda e: e.memset(out.ap, val))

    def recip(self, out, in_):
        self.op("dve", [in_], [out], lambda e: e.reciprocal(out=out.ap, in_=in_.ap))


def build_program(cfg):
    nc = bass.Bass("TRN2", target_bir_lowering=False)
    es = contextlib.ExitStack()
    with es:
        _emit(nc, es, cfg)
    return nc


def _emit(nc, es, cfg):
    P = Prog(nc, es)
    layers = cfg.get("layers", ["A", "F", "B", "F", "A", "F", "B", "F"])

    def din(name, shape):
        return nc.dram_tensor(name, list(shape), F32, kind="ExternalInput").ap()

    def dout(name, shape):
        return nc.dram_tensor(name, list(shape), F32, kind="ExternalOutput").ap()

    xp = din("xp", (SEQ, D))
    xs = din("xs", (NS, D))
    sd = din("sd", (2, NS, H_B, 128, 128))
    sc = din("sc", (2, NS, 3, QKV))
    norm_mix = din("norm_mix", (DEPTH, D))
    norm_ffn = din("norm_ffn", (DEPTH, D))
    norm_final = din("norm_final", (D,))
    a_w_in = din("a_w_in", (2, D, 2 * D_A))
    a_v_norm = din("a_v_norm", (2, D_A))
    a_w_spatial = din("a_w_spatial", (2, H_A, 128, 128))
    a_b_spatial = din("a_b_spatial", (2, H_A, 128))
    a_w_out = din("a_w_out", (2, D_A, D))
    b_w_in = din("b_w_in", (2, D, B_IN))
    b_w_conv = din("b_w_conv", (2, 4, QKV))
    b_a_log = din("b_a_log", (2, H_B))
    b_dt_bias = din("b_dt_bias", (2, H_B))
    b_o_norm = din("b_o_norm", (2, 128))
    b_w_out = din("b_w_out", (2, D, D))
    ffn_w_in = din("ffn_w_in", (DEPTH, D, 2 * D_FF))
    ffn_w_out = din("ffn_w_out", (DEPTH, D_FF, D))

    yp = dout("yp", (SEQ, D))
    ys = dout("ys", (NS, D))
    ndp = dout("ndp", (2, H_B, 128, 128))
    ncp = dout("ncp", (2, 3, QKV))
    nds = dout("nds", (2, NS, H_B, 128, 128))
    ncs = dout("ncs", (2, NS, 3, QKV))
    ncv = dout("ncv", (2, NS, D_A))

    xT = P.alloc(F32, (KC, NT))
    ident_f = P.alloc(F32, (128,))
    ident_b = P.alloc(BF16, (128,))
    ones_b = P.alloc(BF16, (128,))
    ones128_b = P.alloc(BF16, (128,))
    ones_f = P.alloc(F32, (128,))
    epsc = P.alloc(F32, (1,))
    gmix = P.alloc(F32, (DEPTH, KC))
    gffn = P.alloc(F32, (DEPTH, KC))
    NW = 2
    WSLOT = [P.alloc(BF16, (KC, 512)) for _ in range(NW)]
    OSLOT = [P.alloc(BF16, (4, D)) for _ in range(2)]
    wctr = [0]
    octr = [0]

    P.memset(ones_b.full(), 1.0 / D)
    P.memset(ones128_b.full(), 1.0 / 128)
    P.memset(ones_f.full(), 1.0)
    P.memset(epsc.full(), EPS)
    P.memset(ident_f.full(), 0.0)
    P.op("pool", [ones_f.full()], [ident_f.full()],
         lambda g: g.affine_select(out=ident_f.ap, in_=ones_f.ap, pattern=[[-1, 128]], compare_op=ALU.is_equal,
                                   fill=0.0, base=0, channel_multiplier=1))
    P.copy(ident_b.full(), ident_f.full())
    with nc.allow_non_contiguous_dma(reason="tiny gain vectors"):
        P.dma("sp", gmix.full(), norm_mix.rearrange("l (k p) -> p l k", p=128))
        P.dma("sp", gffn.full(), norm_ffn.rearrange("l (k p) -> p l k", p=128))

    def load_w(dram_ap_fn):
        slot = WSLOT[wctr[0] % NW]
        wctr[0] += 1
        dram_ap_fn(slot)
        return slot

    def rmsnorm_T(hT, gain, li, sq, rs):
        for (c0, n) in TILES:
            for k in range(KC):
                P.act(sq[:, k, 0:n], xT[:, k, c0:c0 + n], AF.Square)
            ps = P.bank()
            P.mm(ps[:, 0:n], [(ones_b.full(), sq[:, k, 0:n]) for k in range(KC)])
            P.act(rs[:, 1, 0:n], ps[:, 0:n], AF.Ln, bias=epsc[:, 0:1])
            P.act(rs[:, 1, 0:n], rs[:, 1, 0:n], AF.Exp, scale=-0.5)
            for k in range(KC):
                P.stt(hT[:, k, c0:c0 + n], xT[:, k, c0:c0 + n], gain[:, li, k:k + 1], rs[:, 1, 0:n],
                      ALU.mult, ALU.mult)

    def load_x():
        m = P.mark()
        xin = [P.alloc(F32, (D,)) for _ in range(2)]
        nb = SEQ // 128
        for b in cfg.get('lx_blocks', range(nb + 1)):
            xi = xin[b % 2]
            rows = 128 if b < nb else NS
            src = xp[b * 128:(b + 1) * 128, :] if b < nb else xs
            P.dma("sp", xi[0:rows, :], src)
            for half in range(2):
                ps = P.bank()
                for kk in range(4):
                    k = half * 4 + kk
                    P.transpose_hw(ps[:, kk * 128:kk * 128 + rows], xi[0:rows, k * 128:(k + 1) * 128],
                                ident_f[0:rows, 0:rows])
                for kk in range(4):
                    k = half * 4 + kk
                    eng = "act" if (kk % 2 and not cfg.get("lx_noact")) else "dve"
                    P.copy(xT[:, k, b * 128:b * 128 + rows], ps[:, kk * 128:kk * 128 + rows], eng=eng)
        P.release(m)

    def final_out():
        m = P.mark()
        gfin = P.alloc(F32, (D,))
        P.dma("sp", gfin.full(), norm_final.partition_broadcast(128))
        sqt = P.alloc(F32, (D,))
        st = P.alloc(F32, (4,))
        yo = [P.alloc(F32, (D,)) for _ in range(2)]
        nb = SEQ // 128
        for b in range(nb + 1):
            rows = 128 if b < nb else NS
            ps = P.bank2()
            for k in range(KC):
                P.transpose_hw(ps[0:rows, k * 128:(k + 1) * 128], xT[:, k, b * 128:b * 128 + rows], ident_f.full())
            P.act(sqt[0:rows, :], ps[0:rows, :], AF.Square)
            P.op("dve", [sqt[0:rows, :]], [st[0:rows, 0:1]],
                 lambda e: e.reduce_sum(out=st[0:rows, 0:1].ap, in_=sqt[0:rows, :].ap, axis=AX.X))
            P.act(st[0:rows, 1:2], st[0:rows, 0:1], AF.Sqrt, bias=epsc[0:rows, 0:1], scale=1.0 / D)
            P.recip(st[0:rows, 2:3], st[0:rows, 1:2])
            y = yo[b % 2]
            P.stt(y[0:rows, :], ps[0:rows, :], st[0:rows, 2:3], gfin[0:rows, :], ALU.mult, ALU.mult)
            dst = yp[b * 128:(b + 1) * 128, :] if b < nb else ys
            P.dma("sp", dst, y[0:rows, :])
        P.release(m)

    def ffn(li):
        m = P.mark()
        WS = WSLOT + [P.alloc(BF16, (KC, 512))]
        hT = P.alloc(BF16, (KC, NT))
        sq = P.alloc(BF16, (KC, 512))
        rs = P.alloc(F32, (2, 512))
        rmsnorm_T(hT, gffn, li, sq, rs)
        actb = P.alloc(BF16, (4, NT))
        gs = [P.alloc(F32, (512,)) for _ in range(3)]
        gsc = 0
        win = ffn_w_in[li].rearrange("(k p) n -> p k n", p=128)
        wout = ffn_w_out[li].rearrange("(j p) n -> p j n", p=128)
        quarters = [(0, 4), (4, 4), (8, 4), (12, 4), (16, 4), (20, 2)]

        def issue_w(sb):
            slot = WS[wctr[0] % 3]
            wctr[0] += 1
            c = sb * 256
            P.dma("pool", slot[:, :, 0:256], win[:, :, c:c + 256])
            P.dma("pool", slot[:, :, 256:512], win[:, :, D_FF + c:D_FF + c + 256])
            return slot

        def issue_o(q0, nf):
            slot = OSLOT[octr[0] % 2]
            octr[0] += 1
            P.dma("pool", slot[:, 0:nf, :], wout[:, q0:q0 + nf, :])
            return slot

        sbs = [issue_w(0)]
        oslots = [issue_o(*quarters[0])]
        for qi, (q0, nf) in enumerate(quarters):
            for s in range(nf // 2):
                sb = q0 // 2 + s
                if sb + 1 < 11:
                    sbs.append(issue_w(sb + 1))
                w = sbs[sb]
                for j in range(2):
                    jj = s * 2 + j
                    for (c0, n) in TILES:
                        pg = P.bank()
                        pu = P.bank()
                        P.mm(pg[:, 0:n], [(w[:, k, j * 128:(j + 1) * 128], hT[:, k, c0:c0 + n]) for k in range(KC)])
                        P.mm(pu[:, 0:n], [(w[:, k, 256 + j * 128:256 + (j + 1) * 128], hT[:, k, c0:c0 + n])
                                          for k in range(KC)])
                        g = gs[gsc % 3]
                        gsc += 1
                        P.act(g[:, 0:n], pg[:, 0:n], AF.Silu)
                        P.tt(actb[:, jj, c0:c0 + n], g[:, 0:n], pu[:, 0:n], ALU.mult)
            if qi + 1 < len(quarters):
                oslots.append(issue_o(*quarters[qi + 1]))
            wo = oslots[qi]
            for (c0, n) in TILES:
                for d in range(KC):
                    ps = P.bank()
                    P.mm(ps[:, 0:n], [(wo[:, jj, d * 128:(d + 1) * 128], actb[:, jj, c0:c0 + n]) for jj in range(nf)])
                    P.tt(xT[:, d, c0:c0 + n], xT[:, d, c0:c0 + n], ps[:, 0:n], ALU.add)
        P.release(m)

    def rmsnorm_tile(hTt, gain, li, sq, rs, c0, n, o0=0):
        for k in range(KC):
            P.act(sq[:, k, 0:n], xT[:, k, c0:c0 + n], AF.Square)
        ps = P.bank()
        P.mm(ps[:, 0:n], [(ones_b.full(), sq[:, k, 0:n]) for k in range(KC)])
        P.act(rs[:, 0, 0:n], ps[:, 0:n], AF.Ln, bias=epsc[:, 0:1])
        P.act(rs[:, 0, 0:n], rs[:, 0, 0:n], AF.Exp, scale=-0.5)
        for k in range(KC):
            P.stt(hTt[:, k, o0:o0 + n], xT[:, k, c0:c0 + n], gain[:, li, k:k + 1], rs[:, 0, 0:n],
                  ALU.mult, ALU.mult)

    one_row = P.alloc(BF16, (128,))
    P.memset(one_row.full(), 1.0)

    def mixer_a(li, j):
        m = P.mark()
        WS = WSLOT + [P.alloc(BF16, (KC, 512))]
        NTT = 512 + NS
        hTt = P.alloc(BF16, (KC, NTT))
        rs = P.alloc(F32, (2, 512))
        uT = P.alloc(BF16, (16, NTT))
        sq = Region(P.arena, "S", uT.lo, BF16, (KC, 512))
        vtok = P.alloc(BF16, (4, D_A))
        vn = [P.alloc(BF16, (D_A,)) for _ in range(2)]
        gvb = P.alloc(F32, (D_A,))
        WT = P.alloc(BF16, (H_A, 128))
        wsf = Region(P.arena, "S", vtok.lo, F32, (H_A, 128))
        Wsamp = P.alloc(BF16, (H_A, 16))
        w00 = P.alloc(F32, (H_A,))
        b0col = P.alloc(F32, (H_A,))
        browf = Region(P.arena, "S", vtok.lo + 4096, F32, (H_A * 128,))
        brow = P.alloc(BF16, (H_A, 128))
        sqs = P.alloc(F32, (512,))
        st = P.alloc(F32, (8,))
        sqss = [sqs, P.alloc(F32, (512,))]
        sts = [P.alloc(F32, (8,)) for _ in range(2)]
        vs = P.alloc(F32, (D_A,))
        vsn = P.alloc(F32, (D_A,))
        P.dma("sp", wsf.full(), a_w_spatial[j].rearrange("g t s -> t g s"))
        P.dma("sp", gvb.full(), a_v_norm[j].partition_broadcast(128))
        with nc.allow_non_contiguous_dma(reason="tiny per-group scalars"):
            P.dma("sp", w00[0:16, :], a_w_spatial[j, :, 0, 0].partition_broadcast(16))
            P.dma("sp", b0col.full(), a_b_spatial[j, :, 0].partition_broadcast(128))
        P.dma("sp", browf[0:1, :], a_b_spatial[j].rearrange("g t -> (g t)").partition_broadcast(1))
        P.copy(brow[0:1, :, :], Region(P.arena, "S", browf.lo, F32, (H_A, 128))[0:1, :, :])
        P.op("pool", [wsf.full()], [wsf.full()],
             lambda g: g.affine_select(out=wsf.ap, in_=wsf.ap, pattern=[[0, H_A], [-1, 128]],
                                       compare_op=ALU.is_ge, fill=0.0, base=0, channel_multiplier=1))
        for half in range(2):
            ps = P.bank()
            for gg in range(4):
                P.transpose(ps[:, gg * 128:(gg + 1) * 128], wsf[:, half * 4 + gg, :], ident_f.full())
            P.copy(WT[:, half * 4:half * 4 + 4, :], Region(P.psum, "P", ps.lo, F32, (4, 128)).full())
        for g in range(H_A):
            P.ts(Wsamp[0:16, g, :], ident_f[0:16, 0:16], w00[0:16, g:g + 1], ALU.mult)
        win = a_w_in[j].rearrange("(k p) n -> p k n", p=128)
        wout = a_w_out[j].rearrange("(f p) n -> p f n", p=128)

        for tg in range(4):
            c0 = tg * 512
            subs = [(c0, 512, 0)] + ([(SEQ, NS, 512)] if tg == 3 else [])
            for (cc, n, o0) in subs:
                rmsnorm_tile(hTt, gmix, li, sq, rs, cc, n, o0)
            for fb in range(4):
                slot = WS[wctr[0] % 3]
                wctr[0] += 1
                P.dma("pool", slot.full(), win[:, :, fb * 512:(fb + 1) * 512])
                for fc in range(4):
                    for (cc, n, o0) in subs:
                        ps = P.bank()
                        P.mm(ps[:, 0:n], [(slot[:, k, fc * 128:(fc + 1) * 128], hTt[:, k, o0:o0 + n])
                                          for k in range(KC)])
                        P.act(uT[:, fb * 4 + fc, o0:o0 + n], ps[:, 0:n], AF.Gelu_apprx_tanh)
            for fb in range(4):
                slot = WS[wctr[0] % 3]
                wctr[0] += 1
                P.dma("pool", slot.full(), win[:, :, D_A + fb * 512:D_A + (fb + 1) * 512])
                for c in range(4):
                    ps = P.bank()
                    P.mm(ps.full(), [(hTt[:, k, c * 128:(c + 1) * 128], slot[:, k, :]) for k in range(KC)])
                    P.act(vtok[:, c, fb * 512:(fb + 1) * 512], ps.full(), AF.Gelu_apprx_tanh)
                if tg == 3:
                    ps = P.bank()
                    P.mm(ps[0:NS, :], [(hTt[:, k, 512:512 + NS], slot[:, k, :]) for k in range(KC)])
                    P.act(vs[0:NS, fb * 512:(fb + 1) * 512], ps[0:NS, :], AF.Gelu_apprx_tanh)
            def prep(c):
                stc = sts[c % 2]
                for fb in range(4):
                    sq_ = sqss[fb % 2]
                    P.act(sq_.full(), vtok[:, c, fb * 512:(fb + 1) * 512], AF.Square)
                    P.op("dve", [sq_.full()], [stc[:, fb:fb + 1]],
                         lambda e, fb=fb, sq_=sq_, stc=stc: e.reduce_sum(out=stc[:, fb:fb + 1].ap, in_=sq_.ap, axis=AX.X))
                P.op("dve", [stc[:, 0:4]], [stc[:, 4:5]],
                     lambda e, stc=stc: e.reduce_sum(out=stc[:, 4:5].ap, in_=stc[:, 0:4].ap, axis=AX.X))
                P.act(stc[:, 5:6], stc[:, 4:5], AF.Sqrt, bias=epsc[:, 0:1], scale=1.0 / D_A)
                P.recip(stc[:, 6:7], stc[:, 5:6])
                P.stt(vn[c % 2].full(), vtok[:, c, :], stc[:, 6:7], gvb.full(), ALU.mult, ALU.mult)

            def mixp(c):
                v_n = vn[c % 2]
                for q in range(4):
                    ps = P.bank()
                    for ff in range(4):
                        fc = q * 4 + ff
                        g = fc // 2
                        P.mm(ps[:, ff * 128:(ff + 1) * 128],
                             [(v_n[:, fc * 128:(fc + 1) * 128], WT[:, g, :]),
                              (one_row[0:1, 0:128], brow[0:1, g, :])])
                    P.tt(uT[:, q * 4:q * 4 + 4, c * 128:(c + 1) * 128],
                         Region(P.psum, "P", ps.lo, F32, (4, 128)).full(),
                         uT[:, q * 4:q * 4 + 4, c * 128:(c + 1) * 128], ALU.mult)

            prep(0)
            for c in range(4):
                if c + 1 < 4:
                    prep(c + 1)
                mixp(c)
            if tg == 3:
                for fb in range(4):
                    P.act(sqs[0:NS, :], vs[0:NS, fb * 512:(fb + 1) * 512], AF.Square)
                    P.op("dve", [sqs[0:NS, :]], [st[0:NS, fb:fb + 1]],
                         lambda e, fb=fb: e.reduce_sum(out=st[0:NS, fb:fb + 1].ap, in_=sqs[0:NS, :].ap, axis=AX.X))
                P.op("dve", [st[0:NS, 0:4]], [st[0:NS, 4:5]],
                     lambda e: e.reduce_sum(out=st[0:NS, 4:5].ap, in_=st[0:NS, 0:4].ap, axis=AX.X))
                P.act(st[0:NS, 5:6], st[0:NS, 4:5], AF.Sqrt, bias=epsc[0:NS, 0:1], scale=1.0 / D_A)
                P.recip(st[0:NS, 6:7], st[0:NS, 5:6])
                P.stt(vsn[0:NS, :], vs[0:NS, :], st[0:NS, 6:7], gvb[0:NS, :], ALU.mult, ALU.mult)
                P.dma("sp", ncv[j], vsn[0:NS, :])
                v_n = vn[0]
                P.copy(v_n[0:NS, :], vsn[0:NS, :])
                for q in range(4):
                    ps = P.bank()
                    for ff in range(4):
                        fc = q * 4 + ff
                        g = fc // 2
                        P.mm(ps[:, ff * 128:ff * 128 + NS], [(v_n[0:NS, fc * 128:(fc + 1) * 128], Wsamp[0:NS, g, :])])
                    for ff in range(4):
                        fc = q * 4 + ff
                        g = fc // 2
                        P.stt(uT[:, fc, 512:512 + NS], ps[:, ff * 128:ff * 128 + NS], b0col[:, g:g + 1],
                              uT[:, fc, 512:512 + NS], ALU.add, ALU.mult)
            for (f0, nf) in ((0, 4), (4, 4), (8, 4), (12, 4)):
                wo = OSLOT[octr[0] % 2]
                octr[0] += 1
                P.dma("pool", wo[:, 0:nf, :], wout[:, f0:f0 + nf, :])
                for (cc, n, o0) in subs:
                    for d in range(KC):
                        ps = P.bank()
                        P.mm(ps[:, 0:n], [(wo[:, jj, d * 128:(d + 1) * 128], uT[:, f0 + jj, o0:o0 + n])
                                          for jj in range(nf)])
                        P.tt(xT[:, d, cc:cc + n], xT[:, d, cc:cc + n], ps[:, 0:n], ALU.add)
        P.release(m)

    def R4(ps):
        return Region(P.psum, "P", ps.lo, F32, (4, 128))

    def mixer_b(li, l):
        m = P.mark()
        WSB = WSLOT + [P.alloc(BF16, (KC, 512))]
        BT = 512
        NCH = BT // 128
        NTG = SEQ // BT
        NTT = BT + NS
        hTt = P.alloc(BF16, (KC, NTT))
        rs = P.alloc(F32, (1, BT))
        qn = P.alloc(BF16, (H_B, NTT))
        kn = P.alloc(BF16, (H_B, NTT))
        vT = P.alloc(BF16, (H_B, NTT))
        gateS = Region(P.arena, "S", qn.lo, BF16, (H_B, NTT))
        onT = Region(P.arena, "S", kn.lo, BF16, (H_B, NTT))
        oT = P.alloc(BF16, (H_B, NTT))
        sq = Region(P.arena, "S", oT.lo, BF16, (KC, BT))
        rbb = [P.alloc(BF16, (BT + 4,)) for _ in range(2)]
        dg = [P.alloc(BF16, (4, 128)) for _ in range(2)]
        acc = [P.alloc(BF16, (BT,)) for _ in range(2)]
        accS = [P.alloc(F32, (NS,)) for _ in range(2)]
        sqh = P.alloc(BF16, (BT,))
        halo = P.alloc(BF16, (24, 3))
        wconv = P.alloc(F32, (24, 4))
        wc4 = Region(P.arena, "S", qn.lo, F32, (QKV,))
        nA = P.alloc(F32, (H_B,))
        dtb = P.alloc(F32, (H_B,))
        gocol = P.alloc(F32, (1,))
        Utri = P.alloc(F32, (128,))
        Lmask = P.alloc(BF16, (128,))
        Amask = P.alloc(BF16, (128,))
        onesb4 = P.alloc(BF16, (128,))
        batok = P.alloc(F32, (NCH, 16))
        sc8s = [P.alloc(F32, (17, H_B)) for _ in range(2)]
        sc8 = sc8s[0]
        class CB:
            pass
        CBS = []
        for _ci in range(2):
            B = CB()
            B.base = P.top
            B.gbc = P.alloc(F32, (4, 128))
            B.dtmp = Region(P.arena, "S", B.gbc.lo, F32, (4, 128))
            B.Dm = P.alloc(BF16, (4, 128))
            B.Dms = P.alloc(BF16, (4, 128))
            B.egT = P.alloc(BF16, (4, 128))
            B.qg = P.alloc(BF16, (4, 128))
            B.L = P.alloc(F32, (4, 128))
            B.U = P.alloc(F32, (4, 128))
            B.Pm = P.alloc(F32, (4, 128))
            B.Am = P.alloc(BF16, (4, 128))
            B.TmT = Region(P.arena, "S", B.Am.lo, BF16, (4, 128))
            B.AT = Region(P.arena, "S", B.Dm.lo, BF16, (4, 128))
            B.kbg = P.alloc(BF16, (4, 128))
            B.kd = P.alloc(BF16, (4, 128))
            B.vb = P.alloc(BF16, (4, 128))
            B.utok = P.alloc(BF16, (4, 128))
            B.wT = Region(P.arena, "S", B.vb.lo, BF16, (4, 128))
            B.vnew = B.utok
            CBS.append(B)
        S_f = P.alloc(F32, (H_B, 128))
        S_b = P.alloc(BF16, (H_B, 128))
        rawS = Region(P.arena, "S", OSLOT[0].lo, F32, (24, NS))
        scT = Region(P.arena, "S", OSLOT[0].lo + 2048, F32, (24, 3, NS))
        assert 2048 + scT.nbytes <= OSLOT[0].nbytes
        sct = Region(P.arena, "S", CBS[1].base, F32, (3, 512))
        rowS = Region(P.arena, "S", CBS[0].base + 12288, F32, (512,))
        kq2 = P.alloc(F32, (H_B, NS, 2))
        bas = P.alloc(F32, (16,))
        decb = Region(P.arena, "S", CBS[0].base + 14336, F32, (NS, H_B))
        betab = Region(P.arena, "S", CBS[0].base + 14336 + 512, F32, (NS, H_B))
        kqb = Region(P.arena, "S", CBS[0].base + 14336 + 1024, F32, (H_B, NS))
        rhsd = Region(P.arena, "S", CBS[0].base + 14336 + 1536, F32, (NS, H_B))
        Sin = [Region(P.arena, "S", CBS[0].base + i * 4096, F32, (H_B, 128)) for i in range(2)]
        Snew = Region(P.arena, "S", CBS[0].base + 8192, F32, (H_B, 128))
        dSs = [P.alloc(F32, (6, H_B)) for _ in range(2)]
        Snews = [Snew, Region(P.arena, "S", CBS[1].base + 6144, F32, (H_B, 128))]
        dbc4 = Region(P.arena, "S", CBS[1].base + 10240, F32, (4, 128))
        P.dma("sp", wc4[0:4, :], b_w_conv[l])
        P.dma("sp", nA.full(), b_a_log[l].partition_broadcast(128))
        P.dma("sp", dtb.full(), b_dt_bias[l].partition_broadcast(128))
        with nc.allow_non_contiguous_dma(reason="128-element column"):
            P.dma("sp", gocol.full(), b_o_norm[l].rearrange("(p o) -> p o", o=1))
        ps = P.bank()
        for fc in range(24):
            P.transpose(ps[:, fc * 4:fc * 4 + 4], wc4[0:4, fc * 128:(fc + 1) * 128], ident_f[0:4, 0:4])
        P.copy(wconv.full(), Region(P.psum, "P", ps.lo, F32, (24, 4)).full())
        P.act(nA.full(), nA.full(), AF.Exp)
        P.ts(nA.full(), nA.full(), -1.0, ALU.mult)
        P.op("pool", [ones_f.full()], [Utri.full()],
             lambda g: g.affine_select(out=Utri.ap, in_=ones_f.ap, pattern=[[1, 128]], compare_op=ALU.is_ge,
                                       fill=0.0, base=0, channel_multiplier=-1))
        P.memset(onesb4.full(), 1.0)
        P.op("pool", [onesb4.full()], [Lmask.full()],
             lambda g: g.affine_select(out=Lmask.ap, in_=onesb4.ap, pattern=[[-1, 128]],
                                       compare_op=ALU.is_gt, fill=0.0, base=0, channel_multiplier=1))
        P.op("pool", [onesb4.full()], [Amask.full()],
             lambda g: g.affine_select(out=Amask.ap, in_=onesb4.ap, pattern=[[-1, 128]],
                                       compare_op=ALU.is_ge, fill=0.0, base=0, channel_multiplier=1))
        P.memset(halo.full(), 0.0)
        P.memset(S_f.full(), 0.0)
        P.memset(S_b.full(), 0.0)
        win = b_w_in[l].rearrange("(k p) n -> p k n", p=128)
        wout = b_w_out[l].rearrange("(f p) n -> p f n", p=128)
        rbc = [0]
        sqh2 = Region(P.arena, "S", rbb[0].lo, BF16, (BT,))
        rs2 = Region(P.arena, "S", dg[0].lo, F32, (1, BT))
        assert dg[1].lo == dg[0].lo + 1024 and rs2.nbytes <= 2048
        nrm = [(sqh, rs), (sqh2, rs2)]
        nrc = [0]

        def conv_silu(psr, fcg, n, dst):
            rb = rbb[rbc[0] % 2]
            d = dg[rbc[0] % 2]
            rbc[0] += 1
            P.copy(rb[:, 3:3 + n], psr)
            P.copy(rb[:, 0:3], halo[:, fcg, :])
            for jt in range(4):
                P.ts(d[:, jt, :], ident_b.full(), wconv[:, fcg, jt:jt + 1], ALU.mult)
            ps2 = P.bank()
            P.mm(ps2[:, 0:n], [(d[:, jt, :], rb[:, jt:jt + n]) for jt in range(4)])
            P.copy(halo[:, fcg, :], rb[:, n:n + 3])
            P.act(dst, ps2[:, 0:n], AF.Silu)

        def rsqrt_act(out, in_, np_=128):
            P.act(out, in_, AF.Ln, bias=epsc[0:np_, 0:1])
            P.act(out, out, AF.Exp, scale=-0.5)

        def l2n(src, dst, n, scale):
            sq_, rs_ = nrm[nrc[0] % 2]
            nrc[0] += 1
            P.act(sq_[:, 0:n], src, AF.Square)
            ps = P.bank()
            P.mm(ps[:, 0:n], [(one_row.full(), sq_[:, 0:n])])
            rsqrt_act(rs_[:, 0, 0:n], ps[:, 0:n])
            P.stt(dst, src, scale, rs_[:, 0, 0:n], ALU.mult, ALU.mult)

        def gate_decay(sc, np_, a_raw, gout):
            r = lambda i: sc[0:np_, i, :]
            P.tt(r(10), a_raw, dtb[0:np_, :], ALU.add)
            P.act(r(11), r(10), AF.Exp)
            P.act(r(12), r(11), AF.Ln, bias=1.0)
            P.ts(r(13), r(11), 2.0, ALU.add)
            P.recip(r(13), r(13))
            P.tt(r(13), r(13), r(11), ALU.mult)
            P.tt(r(14), r(13), r(13), ALU.mult)
            P.ts(r(15), r(14), 1.0 / 9, ALU.mult, 1.0 / 7, ALU.add)
            P.tt(r(15), r(15), r(14), ALU.mult)
            P.ts(r(15), r(15), 1.0 / 5, ALU.add)
            P.tt(r(15), r(15), r(14), ALU.mult)
            P.ts(r(15), r(15), 1.0 / 3, ALU.add)
            P.tt(r(15), r(15), r(14), ALU.mult)
            P.ts(r(15), r(15), 1.0, ALU.add)
            P.tt(r(15), r(15), r(13), ALU.mult)
            P.ts(r(15), r(15), 2.0, ALU.mult)
            P.ts(r(16), r(11), 1.0, ALU.is_le)
            P.tt(r(15), r(15), r(12), ALU.subtract)
            P.tt(r(15), r(15), r(16), ALU.mult)
            P.tt(r(15), r(15), r(12), ALU.add)
            P.tt(gout, r(15), nA[0:np_, :], ALU.mult)

        def sample_delta():
            with nc.allow_non_contiguous_dma(reason="hbm->hbm state row copy"):
                P.dma("sp", ncs[l][:, 0:2, :], sc[l][:, 1:3, :])
            P.dma("sp", Sin[0].full(), sd[l, 0].rearrange("h k v -> k h v"))
            for fcg in range(24):
                a = accS[fcg % 2]
                P.ts(a[:, 0:NS], scT[:, fcg, 0, :], wconv[:, fcg, 0:1], ALU.mult)
                P.stt(a[:, 0:NS], scT[:, fcg, 1, :], wconv[:, fcg, 1:2], a[:, 0:NS], ALU.mult, ALU.add)
                P.stt(a[:, 0:NS], scT[:, fcg, 2, :], wconv[:, fcg, 2:3], a[:, 0:NS], ALU.mult, ALU.add)
                P.stt(a[:, 0:NS], rawS[:, fcg, :], wconv[:, fcg, 3:4], a[:, 0:NS], ALU.mult, ALU.add)
                P.act(rawS[:, fcg, :], a[:, 0:NS], AF.Silu)
            for fcg in range(16):
                h = fcg % 8
                l2n(rawS[:, fcg, :], kq2[:, h, :, 1 if fcg < 8 else 0], NS, 128.0 ** -0.5 if fcg < 8 else 1.0)
            r = lambda i: sc8[0:NS, i, :]
            P.act(r(0), bas[0:NS, 0:8], AF.Exp, scale=-1.0)
            P.ts(r(0), r(0), 1.0, ALU.add)
            P.recip(r(0), r(0))
            gate_decay(sc8, NS, bas[0:NS, 8:16], r(2))
            P.act(r(3), r(2), AF.Exp)
            for (src, dst) in ((r(3), decb), (r(0), betab)):
                for t in range(NS):
                    P.ts(rhsd[0:NS, t, :], src, ident_f[0:NS, t:t + 1], ALU.mult)
                ps = P.bank()
                P.mm(ps[:, 0:NS * H_B], [(ones_f[0:NS, 0:128], Region(P.arena, "S", rhsd.lo, F32, (NS * H_B,))[0:NS, :])])
                P.copy(dst.full(), Region(P.psum, "P", ps.lo, F32, (NS, H_B)).full())
            P.tt(kqb.full(), kq2[:, :, :, 0], kq2[:, :, :, 1], ALU.mult)
            ps = P.bank()
            P.mm(ps[:, 0:H_B * NS], [(ones_f.full(), Region(P.arena, "S", kqb.lo, F32, (H_B * NS,)).full())])
            P.copy(kqb.full(), Region(P.psum, "P", ps.lo, F32, (H_B, NS)).full())
            for t in range(NS):
                Si = Sin[t % 2]
                Sn = Snews[t % 2]
                dS = dSs[t % 2]
                if t + 1 < NS:
                    P.dma("sp", Sin[(t + 1) % 2].full(), sd[l, t + 1].rearrange("h k v -> k h v"))
                ps = P.bank()
                for h in range(H_B):
                    P.mm(ps[:, h * 2:h * 2 + 2], [(Si[:, h, :], kq2[:, h, t, :])])
                pv2 = Region(P.psum, "P", ps.lo, F32, (H_B, 2))
                d_ = lambda i: dS[:, i, :]
                P.tt(d_(0), pv2[:, :, 0], decb[:, t, :], ALU.mult)
                P.tt(d_(0), rawS[:, 16:24, t], d_(0), ALU.subtract)
                P.tt(d_(0), d_(0), betab[:, t, :], ALU.mult)
                P.tt(d_(1), pv2[:, :, 1], decb[:, t, :], ALU.mult)
                P.tt(d_(2), d_(0), kqb[:, :, t], ALU.mult)
                P.tt(oT[:, :, BT + t], d_(1), d_(2), ALU.add)
                for half in range(2):
                    pb = P.bank()
                    dv = dS[:, 0, half * 4:half * 4 + 4]
                    P.tt(dbc4.full(), bcm4(ident_f),
                         V(dv.ap.unsqueeze(2).broadcast_to([128, 4, 128]), dv.space, dv.lo, dv.hi), ALU.mult)
                    P.mm(pb.full(), [(ones_f.full(), Region(P.arena, "S", dbc4.lo, F32, (512,)).full())])
                    for i in range(4):
                        h = half * 4 + i
                        P.act(Sn[:, h, :], Si[:, h, :], AF.Identity, scale=decb[:, t, h:h + 1])
                        P.stt(Sn[:, h, :], R4(pb)[:, i, :], kq2[:, h, t, 0:1], Sn[:, h, :], ALU.mult, ALU.add)
                P.dma("sp", nds[l, t].rearrange("h k v -> k h v"), Sn.full())

        def chain(cs, hh, B, sc8):
            hs = [hh * 4 + i for i in range(4)]
            P.tt(B.gbc.full(), bcm4(Utri), bc4(sc8, 2, hs[0]), ALU.mult)
            psg = P.bank()
            P.mm(psg.full(), [(ones_f.full(), Region(P.arena, "S", B.gbc.lo, F32, (512,)).full())])
            yield
            P.tt(B.dtmp.full(), R4(psg).full(), bc4(sc8, 3, hs[0]), ALU.subtract)
            P.ts(B.dtmp.full(), B.dtmp.full(), 0.0, ALU.max)
            P.act(B.egT.full(), R4(psg).full(), AF.Exp)
            P.act(B.Dm.full(), B.dtmp.full(), AF.Exp, scale=-1.0)
            P.tt(B.Dms.full(), B.Dm.full(), bcm4(Lmask), ALU.mult, eng="pool")
            P.tt(B.Dms.full(), B.Dms.full(), bc4(sc8, 0, hs[0]), ALU.mult, eng="pool")
            P.tt(B.Dm.full(), B.Dm.full(), bcm4(Amask), ALU.mult)
            P.tt(B.qg.full(), qn[:, hs[0]:hs[0] + 4, cs], B.egT.full(), ALU.mult, eng="pool")
            pkk = P.bank()
            for i, h in enumerate(hs):
                P.mm(R4(pkk)[:, i, :], [(kn[:, h, cs], kn[:, h, cs])])
            pqk = P.bank()
            for i, h in enumerate(hs):
                P.mm(R4(pqk)[:, i, :], [(qn[:, h, cs], kn[:, h, cs])])
            yield
            P.tt(B.L.full(), R4(pkk).full(), B.Dms.full(), ALU.mult)
            P.tt(B.Am.full(), R4(pqk).full(), B.Dm.full(), ALU.mult)
            pu = P.bank()
            for i in range(4):
                P.transpose_hw(R4(pu)[:, i, :], B.L[:, i, :], ident_f.full())
            pa = P.bank()
            for i in range(4):
                P.transpose(R4(pa)[:, i, :], B.Am[:, i, :], ident_b.full())
            yield
            P.copy(B.U.full(), R4(pu).full(), eng="act")
            P.tt(B.Pm.full(), bcm4(ident_b), R4(pu).full(), ALU.subtract)
            P.copy(B.AT.full(), R4(pa).full(), eng="act")
            for stp in range(6):
                p1 = P.bank()
                for i in range(4):
                    P.mm(R4(p1)[:, i, :], [(B.U[:, i, :], B.L[:, i, :])])
                yield
                P.copy(B.L.full(), R4(p1).full(), eng="act")
                p3 = P.bank()
                for i in range(4):
                    P.mm(R4(p3)[:, i, :], [(B.L[:, i, :], B.Pm[:, i, :])])
                if stp < 5:
                    p2 = P.bank()
                    for i in range(4):
                        P.transpose_hw(R4(p2)[:, i, :], B.L[:, i, :], ident_f.full())
                yield
                P.tt(B.Pm.full(), B.Pm.full(), R4(p3).full(), ALU.add)
                if stp < 5:
                    P.copy(B.U.full(), R4(p2).full())
            P.copy(B.TmT.full(), B.Pm.full(), eng="act")
            pk = P.bank()
            for i, h in enumerate(hs):
                P.transpose(R4(pk)[:, i, :], kn[:, h, cs], ident_b.full())
            pv = P.bank()
            for i, h in enumerate(hs):
                P.transpose(R4(pv)[:, i, :], vT[:, h, cs], ident_b.full())
            yield
            P.tt(B.kbg.full(), R4(pk).full(), bc4(sc8, 7, hs[0]), ALU.mult)
            P.tt(B.kd.full(), R4(pk).full(), bc4(sc8, 6, hs[0]), ALU.mult)
            P.tt(B.vb.full(), R4(pv).full(), bc4(sc8, 0, hs[0]), ALU.mult)
            pu2 = P.bank()
            for i in range(4):
                P.mm(R4(pu2)[:, i, :], [(B.TmT[:, i, :], B.vb[:, i, :])])
            pw = P.bank()
            for i in range(4):
                P.mm(R4(pw)[:, i, :], [(B.kbg[:, i, :], B.TmT[:, i, :])])
            yield
            P.copy(B.utok.full(), R4(pu2).full(), eng="act")
            P.copy(B.wT.full(), R4(pw).full())
            pws = P.bank()
            for i, h in enumerate(hs):
                P.mm(R4(pws)[:, i, :], [(B.wT[:, i, :], S_b[:, h, :])])
            yield
            P.tt(B.vnew.full(), B.utok.full(), R4(pws).full(), ALU.subtract)
            po = P.bank()
            for i, h in enumerate(hs):
                P.mm(R4(po)[:, i, :], [(S_b[:, h, :], B.qg[:, i, :]), (B.vnew[:, i, :], B.AT[:, i, :])])
            pS = P.bank()
            for i, h in enumerate(hs):
                P.mm(R4(pS)[:, i, :], [(B.kd[:, i, :], B.vnew[:, i, :])])
            yield
            P.copy(oT[:, hs[0]:hs[0] + 4, cs], R4(po).full(), eng="act")
            P.tt(S_f[:, hs[0]:hs[0] + 4, :], S_f[:, hs[0]:hs[0] + 4, :], bc4(sc8, 5, hs[0]), ALU.mult)
            P.tt(S_f[:, hs[0]:hs[0] + 4, :], S_f[:, hs[0]:hs[0] + 4, :], R4(pS).full(), ALU.add)
            P.copy(S_b[:, hs[0]:hs[0] + 4, :], S_f[:, hs[0]:hs[0] + 4, :], eng="act")

        for tg in range(NTG):
            c0 = tg * BT
            last = (tg == NTG - 1)
            rmsnorm_tile(hTt, gmix, li, sq, rs, c0, BT, 0)
            if last:
                rmsnorm_tile(hTt, gmix, li, sq, rs, SEQ, NS, BT)
            pend = [None]
            for fb in range(6):
                slot = WSB[wctr[0] % 3]
                wctr[0] += 1
                P.dma("pool", slot.full(), win[:, :, fb * 512:(fb + 1) * 512])
                for fc in range(4):
                    fcg = fb * 4 + fc
                    ps = P.bank()
                    P.mm(ps[:, 0:BT], [(slot[:, k, fc * 128:(fc + 1) * 128], hTt[:, k, 0:BT]) for k in range(KC)])
                    h = fcg % 8
                    if pend[0] is not None:
                        pend[0]()
                    pend[0] = (lambda ps=ps, fcg=fcg, h=h, fb=fb:
                               conv_silu(ps[:, 0:BT], fcg, BT, (qn if fb < 2 else kn if fb < 4 else vT)[:, h, 0:BT]))
                if fb == 5:
                    pend[0]()
                    pend[0] = None
                if last:
                    ps = P.bank()
                    P.mm(ps[0:3, :], [(hTt[:, k, BT - 3:BT], slot[:, k, :]) for k in range(KC)])
                    P.copy(rowS[0:3, :], ps[0:3, :])
                    P.dma("sp", ncp[l][:, fb * 512:(fb + 1) * 512], rowS[0:3, :])
                    ps = P.bank()
                    for fc in range(4):
                        P.mm(ps[:, fc * NS:(fc + 1) * NS],
                             [(slot[:, k, fc * 128:(fc + 1) * 128], hTt[:, k, BT:BT + NS]) for k in range(KC)])
                    P.copy(rawS[:, fb * 4:fb * 4 + 4, :], Region(P.psum, "P", ps.lo, F32, (4, NS)).full())
                    ps = P.bank()
                    P.mm(ps[0:NS, :], [(hTt[:, k, BT:BT + NS], slot[:, k, :]) for k in range(KC)])
                    P.copy(rowS[0:NS, :], ps[0:NS, :])
                    P.dma("sp", ncs[l][:, 2, fb * 512:(fb + 1) * 512], rowS[0:NS, :])
                    P.dma("sp", sct[0:NS, :, :], sc[l][:, :, fb * 512:(fb + 1) * 512])
                    ps = P.bank()
                    for fc in range(4):
                        for jt in range(3):
                            P.transpose(ps[:, (fc * 3 + jt) * NS:(fc * 3 + jt + 1) * NS],
                                        sct[0:NS, jt, fc * 128:(fc + 1) * 128], ident_f[0:NS, 0:NS])
                    P.copy(scT[:, fb * 4:fb * 4 + 4, :, :], Region(P.psum, "P", ps.lo, F32, (4, 3, NS)).full())
            for (buf, scl) in ((qn, 128.0 ** -0.5), (kn, 1.0)):
                P.act(sq[:, :, 0:BT], buf[:, :, 0:BT], AF.Square)
                for h in range(H_B):
                    ps = P.bank()
                    P.mm(ps[:, 0:BT], [(one_row.full(), sq[:, h, 0:BT])])
                    sq_, rs_ = nrm[nrc[0] % 2]
                    nrc[0] += 1
                    rsqrt_act(rs_[:, 0, 0:BT], ps[:, 0:BT])
                    P.stt(buf[:, h, 0:BT], buf[:, h, 0:BT], scl, rs_[:, 0, 0:BT], ALU.mult, ALU.mult)
            slot = WSB[wctr[0] % 3]
            wctr[0] += 1
            P.dma("pool", slot[:, :, 0:16], win[:, :, 4096:4112])
            for c in range(NCH):
                ps = P.bank()
                P.mm(ps[:, 0:16], [(hTt[:, k, c * 128:(c + 1) * 128], slot[:, k, 0:16]) for k in range(KC)])
                P.copy(batok[:, c, :], ps[:, 0:16])
            if last:
                ps = P.bank()
                P.mm(ps[0:NS, 0:16], [(hTt[:, k, BT:BT + NS], slot[:, k, 0:16]) for k in range(KC)])
                P.copy(bas[0:NS, :], ps[0:NS, 0:16])
            for c in range(NCH):
                cs = slice(c * 128, (c + 1) * 128)
                sc8 = sc8s[c % 2]
                beta = sc8[:, 0, :]
                P.act(beta, batok[:, c, 0:8], AF.Exp, scale=-1.0)
                P.ts(beta, beta, 1.0, ALU.add)
                P.recip(beta, beta)
                gtok = sc8[:, 2, :]
                gate_decay(sc8, 128, batok[:, c, 8:16], gtok)
                ps = P.bank()
                P.mm(ps[:, 0:8], [(Utri.full(), gtok)])
                P.mm(ps[:, 8:16], [(ones_f.full(), gtok)])
                gct = sc8[:, 3, :]
                P.copy(gct, ps[:, 0:8])
                P.act(sc8[:, 4, :], ps[:, 0:8], AF.Exp)
                P.act(sc8[:, 5, :], ps[:, 8:16], AF.Exp)
                P.tt(sc8[:, 6, :], ps[:, 8:16], gct, ALU.subtract)
                P.act(sc8[:, 6, :], sc8[:, 6, :], AF.Exp)
                P.tt(sc8[:, 7, :], beta, sc8[:, 4, :], ALU.mult)
                gens = [chain(cs, hh, CBS[hh], sc8) for hh in range(2)]
                while gens:
                    for g in list(gens):
                        try:
                            next(g)
                        except StopIteration:
                            gens.remove(g)
            if last:
                P.dma("sp", ndp[l].rearrange("h k v -> k h v"), S_f.full())
                sample_delta()
            n = BT
            for fb in (6, 7):
                slot = WSB[wctr[0] % 3]
                wctr[0] += 1
                P.dma("pool", slot.full(), win[:, :, fb * 512:(fb + 1) * 512])
                for fc in range(4):
                    ps = P.bank()
                    P.mm(ps[:, 0:BT], [(slot[:, k, fc * 128:(fc + 1) * 128], hTt[:, k, 0:BT]) for k in range(KC)])
                    P.act(gateS[:, (fb - 6) * 4 + fc, 0:BT], ps[:, 0:BT], AF.Silu)
                    if last:
                        ps = P.bank()
                        P.mm(ps[:, 0:NS], [(slot[:, k, fc * 128:(fc + 1) * 128], hTt[:, k, BT:BT + NS])
                                           for k in range(KC)])
                        P.act(gateS[:, (fb - 6) * 4 + fc, BT:BT + NS], ps[:, 0:NS], AF.Silu)
            subs = [(c0, BT, 0)] + ([(SEQ, NS, BT)] if last else [])
            for h in range(H_B):
                for (cc, n, o0) in subs:
                    sq_, rs_ = nrm[nrc[0] % 2]
                    ac_ = acc[nrc[0] % 2]
                    nrc[0] += 1
                    P.act(sq_[:, 0:n], oT[:, h, o0:o0 + n], AF.Square)
                    ps = P.bank()
                    P.mm(ps[:, 0:n], [(ones128_b.full(), sq_[:, 0:n])])
                    rsqrt_act(rs_[:, 0, 0:n], ps[:, 0:n])
                    P.stt(ac_[:, 0:n], oT[:, h, o0:o0 + n], gocol[:, 0:1], rs_[:, 0, 0:n], ALU.mult, ALU.mult)
                    P.tt(onT[:, h, o0:o0 + n], ac_[:, 0:n], gateS[:, h, o0:o0 + n], ALU.mult)
            for (f0, nf) in ((0, 4), (4, 4)):
                wo = OSLOT[octr[0] % 2]
                octr[0] += 1
                P.dma("pool", wo[:, 0:nf, :], wout[:, f0:f0 + nf, :])
                for (cc, n, o0) in subs:
                    for d in range(KC):
                        ps = P.bank()
                        P.mm(ps[:, 0:n], [(wo[:, jj, d * 128:(d + 1) * 128], onT[:, f0 + jj, o0:o0 + n])
                                          for jj in range(nf)])
                        P.tt(xT[:, d, cc:cc + n], xT[:, d, cc:cc + n], ps[:, 0:n], ALU.add)
        P.release(m)

    def gtok_col(sc8, row, h):
        return sc8[:, row, h:h + 1]

    def bc4(sc8, row, h0):
        v = sc8[:, row, h0:h0 + 4]
        return V(v.ap.unsqueeze(2).broadcast_to([128, 4, 128]), v.space, v.lo, v.hi)

    def bcm4(r):
        v = r.full()
        return V(v.ap.unsqueeze(1).broadcast_to([128, 4, 128]), v.space, v.lo, v.hi)

    def ones4():
        v = ones_f.full()
        return V(v.ap.unsqueeze(1).broadcast_to([128, 4, 128]), v.space, v.lo, v.hi)

    if not cfg.get("skip_load"):
        load_x()
    li_of = {"A": 0, "B": 0, "F": 0}
    lidx = 0
    for kind in layers:
        if kind == "F":
            ffn(cfg.get("ffn_li", [0, 1, 2, 3])[li_of["F"]])
            li_of["F"] += 1
        elif kind == "B":
            mixer_b(2 * li_of["B"] + 1, li_of["B"])
            li_of["B"] += 1
        elif kind == "A":
            mixer_a(2 * li_of["A"], li_of["A"])
            li_of["A"] += 1
    if not cfg.get("skip_final"):
        final_out()
    P.finish()
    cfg["stats"] = dict(nops=P.nops, nwaits=P.nwaits, cnt=dict(P.cnt), dn=dict(P.dn), top=P.top)


def kernel(**inputs):
    cfg = {}
    nc = build_program(cfg)
    in_maps = []
    f = lambda a: np.ascontiguousarray(np.asarray(a, dtype=np.float32))
    shared = {k: f(inputs[k]) for k in (
        "norm_mix", "norm_ffn", "norm_final", "a_w_in", "a_v_norm", "a_w_spatial", "a_b_spatial", "a_w_out",
        "b_w_in", "b_w_conv", "b_a_log", "b_dt_bias", "b_o_norm", "b_w_out", "ffn_w_in", "ffn_w_out")}
    x_prompt = f(inputs["x_prompt"])
    x_sample = f(inputs["x_sample"])
    state_delta = f(inputs["state_delta"])
    state_conv = f(inputs["state_conv"])
    for c in range(NCORES):
        m = dict(shared)
        m["xp"] = np.ascontiguousarray(x_prompt[c])
        m["xs"] = np.ascontiguousarray(x_sample[c * NS:(c + 1) * NS, 0, :])
        m["sd"] = np.ascontiguousarray(state_delta[:, c * NS:(c + 1) * NS])
        m["sc"] = np.ascontiguousarray(state_conv[:, c * NS:(c + 1) * NS])
        in_maps.append(m)
    res = run_bass_kernel_spmd(nc, in_maps, core_ids=list(range(NCORES)))
    R = res.results
    y_prompt = np.stack([R[c]["yp"] for c in range(NCORES)], axis=0)
    y_sample = np.concatenate([R[c]["ys"] for c in range(NCORES)], axis=0)[:, None, :]
    ndp = np.stack([R[c]["ndp"] for c in range(NCORES)], axis=1)
    ncp = np.stack([R[c]["ncp"] for c in range(NCORES)], axis=1)
    nds = np.concatenate([R[c]["nds"] for c in range(NCORES)], axis=1)
    ncs = np.concatenate([R[c]["ncs"] for c in range(NCORES)], axis=1)
    ncv = np.concatenate([R[c]["ncv"] for c in range(NCORES)], axis=1)[:, :, None, :]
    return (y_prompt.astype(np.float32), y_sample.astype(np.float32), ndp.astype(np.float32),
            ncp.astype(np.float32), nds.astype(np.float32), ncs.astype(np.float32), ncv.astype(np.float32))
```

```python
import contextlib
import numpy as np
import concourse.bass as bass
import concourse.mybir as mybir
from concourse.bass_utils import run_bass_kernel_spmd

F32 = mybir.dt.float32
BF16 = mybir.dt.bfloat16
AF = mybir.ActivationFunctionType
ALU = mybir.AluOpType
AX = mybir.AxisListType

NCORES = 8
D = 1024
KC = 8
SEQ = 2048
NS = 16
NT = SEQ + NS
DEPTH = 4
D_A = 2048
H_A = 8
D_FF = 2816
H_B = 8
QKV = 3072
B_IN = 4112
EPS = 1e-6
TILES = [(0, 512), (512, 512), (1024, 512), (1536, 512), (2048, NS)]
ESZ = {F32: 4, BF16: 2}


class V:
    __slots__ = ("ap", "space", "lo", "hi")

    def __init__(self, ap, space, lo, hi):
        self.ap, self.space, self.lo, self.hi = ap, space, lo, hi


class Region:
    def __init__(self, base, space, byte_lo, dtype, shape, parts=128):
        self.space, self.lo, self.dtype, self.shape, self.parts = space, byte_lo, dtype, tuple(shape), parts
        es = ESZ[dtype]
        n = int(np.prod(shape))
        self.nbytes = n * es
        assert byte_lo % 4 == 0
        ap = base[0:parts, byte_lo // 4:(byte_lo + self.nbytes + 3) // 4]
        if dtype != F32:
            ap = ap.bitcast(dtype)
        if len(shape) > 1:
            names = " ".join("d%d" % i for i in range(len(shape)))
            kw = {"d%d" % i: shape[i] for i in range(1, len(shape))}
            ap = ap.rearrange("p (%s) -> p %s" % (names, names), **kw)
        self.ap = ap
        st = [1] * len(shape)
        for i in range(len(shape) - 2, -1, -1):
            st[i] = st[i + 1] * shape[i + 1]
        self.strides = st

    def __getitem__(self, idx):
        if not isinstance(idx, tuple):
            idx = (idx,)
        idx = idx + (slice(None),) * (1 + len(self.shape) - len(idx))
        ap = self.ap[idx]
        es = ESZ[self.dtype]
        lo = 0
        hi = 0
        for i, ix in enumerate(idx[1:]):
            if isinstance(ix, slice):
                a = 0 if ix.start is None else ix.start
                b = self.shape[i] if ix.stop is None else ix.stop
                assert ix.step in (None, 1) and 0 <= a < b <= self.shape[i], (ix, self.shape)
            else:
                a, b = ix, ix + 1
                assert 0 <= a < self.shape[i]
            lo += a * self.strides[i]
            hi += (b - 1) * self.strides[i]
        blo, bhi = self.lo + lo * es, self.lo + (hi + 1) * es
        if self.space == "P":
            blo = blo // 2048 * 2048
            bhi = (bhi + 2047) // 2048 * 2048
        return V(ap, self.space, blo, bhi)

    def full(self):
        return self[(slice(None),)]


class Prog:
    def __init__(self, nc, es, arena_bytes=207 * 1024):
        self.nc = nc
        self.es = es
        self.arena = es.enter_context(nc.sbuf_tensor("arena", [128, arena_bytes // 4], F32))
        self.psum = es.enter_context(nc.psum_tensor("psum", [128, 4096], F32))
        self.arena_bytes = arena_bytes
        self.top = 0
        self.eng = {"pe": nc.tensor, "act": nc.scalar, "dve": nc.vector, "pool": nc.gpsimd, "sp": nc.sync}
        self.sem = {e: es.enter_context(nc.semaphore("c_" + e)) for e in self.eng}
        self.cnt = {e: 0 for e in self.eng}
        self.NDS = 8
        self.dsem = {q: [es.enter_context(nc.semaphore("d_%s%d" % (q, i))) for i in range(self.NDS)]
                     for q in ("sp", "pool")}
        self.dn = {q: 0 for q in ("sp", "pool")}
        self.waited = {e: {} for e in self.eng}
        self.recs = {"S": [], "P": []}
        self.bank_rr = 0
        self.nwaits = 0
        self.nops = 0

    def alloc(self, dtype, shape, parts=128):
        n = int(np.prod(shape)) * ESZ[dtype]
        n = (n + 31) // 32 * 32
        lo = self.top
        self.top += n
        assert self.top <= self.arena_bytes, ("SBUF arena overflow", self.top)
        return Region(self.arena, "S", lo, dtype, shape, parts)

    def mark(self):
        return self.top

    def release(self, m):
        self.top = m

    def bank(self, dtype=F32, shape=None, parts=128, b=None):
        if b is None:
            b = self.bank_rr
            self.bank_rr = (self.bank_rr + 1) % 8
        if shape is None:
            shape = (2048 // ESZ[dtype],)
        return Region(self.psum, "P", b * 2048, dtype, shape, parts)

    def bank2(self, dtype=F32, shape=None, parts=128):
        if self.bank_rr % 2:
            self.bank_rr = (self.bank_rr + 1) % 8
        b = self.bank_rr
        self.bank_rr = (self.bank_rr + 2) % 8
        if shape is None:
            shape = (4096 // ESZ[dtype],)
        return Region(self.psum, "P", b * 2048, dtype, shape, parts)

    def _collect(self, reads, writes, me=None):
        deps = {}
        for lst, isw in ((reads, False), (writes, True)):
            for v in lst:
                isp = v.space == "P"
                for r in self.recs[v.space]:
                    if r[0] < v.hi and v.lo < r[1] and (isw or r[4] or (isp and r[2] != me)):
                        k = r[2]
                        if deps.get(k, (None, 0))[1] < r[3]:
                            deps[k] = (r[5], r[3])
        return deps

    def _record(self, reads, writes, semkey, sem, val):
        for v in writes:
            rl = self.recs[v.space]
            rl[:] = [r for r in rl if not (v.lo <= r[0] and r[1] <= v.hi)]
            rl.append([v.lo, v.hi, semkey, val, True, sem])
        for v in reads:
            rl = self.recs[v.space]
            for r in rl:
                if r[0] == v.lo and r[1] == v.hi and r[2] == semkey and not r[4]:
                    r[3] = val
                    break
            else:
                rl.append([v.lo, v.hi, semkey, val, False, sem])

    def _waits(self, e, deps, skip_self=False):
        w = self.waited[e]
        for k, (sem, val) in deps.items():
            if skip_self and k == e:
                continue
            if w.get(k, 0) < val:
                self.eng[e].wait_ge(sem, val)
                w[k] = val
                self.nwaits += 1

    def op(self, e, reads, writes, fn):
        deps = self._collect(reads, writes, e)
        self._waits(e, deps, skip_self=(e == "pe"))
        ins = fn(self.eng[e])
        self.cnt[e] += 1
        ins.then_inc(self.sem[e], 1)
        self._record(reads, writes, e, self.sem[e], self.cnt[e])
        self.nops += 1

    def dma(self, q, out, in_, out_v=None, in_v=None):
        reads, writes = [], []
        if isinstance(in_, V):
            reads.append(in_)
            in_ = in_.ap
        if isinstance(out, V):
            writes.append(out)
            out = out.ap
        n = self.dn[q]
        s = self.dsem[q][n % self.NDS]
        tgt = 16 * (n // self.NDS + 1)
        self.dn[q] += 1
        key = "%s_d%d" % (q, n % self.NDS)
        deps = self._collect(reads, writes)
        if tgt > 16:
            deps[key] = (s, max(deps.get(key, (None, 0))[1], tgt - 16))
        self._waits(q, deps)
        self.eng[q].dma_start(out=out, in_=in_).then_inc(s, 16)
        self._record(reads, writes, key, s, tgt)

    def finish(self):
        for q in ("sp", "pool"):
            for i in range(self.NDS):
                n = self.dn[q]
                cnt = n // self.NDS + (1 if i < n % self.NDS else 0)
                if cnt:
                    self.nc.sync.wait_ge(self.dsem[q][i], 16 * cnt)

    def mm(self, out, pairs, extra_reads=()):
        reads = list(extra_reads)
        for l, r in pairs:
            reads += [l, r]
        n = len(pairs)

        def fn(pe):
            ins = None
            for i, (l, r) in enumerate(pairs):
                ins = pe.matmul(out.ap, l.ap, r.ap, start=(i == 0), stop=(i == n - 1))
            return ins
        self.op("pe", reads, [out], fn)

    def transpose(self, out, in_, ident):
        self.op("pe", [in_, ident], [out], lambda pe: pe.matmul(out.ap, in_.ap, ident.ap, start=True, stop=True))

    def transpose_hw(self, out, in_, ident):
        self.op("pe", [in_, ident], [out], lambda pe: pe.transpose(out.ap, in_.ap, ident.ap))

    def act(self, out, in_, func, bias=None, scale=1.0, accum=None, eng="act"):
        reads = [in_]
        kw = {}
        if bias is not None:
            if isinstance(bias, V):
                reads.append(bias)
                kw["bias"] = bias.ap
            else:
                kw["bias"] = bias
        if isinstance(scale, V):
            reads.append(scale)
            kw["scale"] = scale.ap
        else:
            kw["scale"] = scale
        writes = [out]
        if accum is not None:
            writes.append(accum)
            kw["accum_out"] = accum.ap
        self.op("act", reads, writes, lambda a: a.activation(out=out.ap, in_=in_.ap, func=func, **kw))

    def tt(self, out, a, b, op, eng="dve"):
        self.op(eng, [a, b], [out], lambda e: e.tensor_tensor(out=out.ap, in0=a.ap, in1=b.ap, op=op))

    def ts(self, out, a, s1, op0, s2=None, op1=None, eng="dve"):
        reads = [a]
        s1a, s2a = s1, s2
        if isinstance(s1, V):
            reads.append(s1)
            s1a = s1.ap
        if isinstance(s2, V):
            reads.append(s2)
            s2a = s2.ap
        kw = {}
        if op1 is not None:
            kw["op1"] = op1
        self.op(eng, reads, [out],
                lambda e: e.tensor_scalar(out=out.ap, in0=a.ap, scalar1=s1a, scalar2=s2a, op0=op0, **kw))

    def stt(self, out, a, s, b, op0, op1, eng="dve"):
        reads = [a, b]
        sa = s
        if isinstance(s, V):
            reads.append(s)
            sa = s.ap
        self.op(eng, reads, [out],
                lambda e: e.scalar_tensor_tensor(out=out.ap, in0=a.ap, scalar=sa, in1=b.ap, op0=op0, op1=op1))

    def copy(self, out, in_, eng="dve"):
        if eng == "act":
            self.op("act", [in_], [out], lambda a: a.copy(out=out.ap, in_=in_.ap))
        else:
            self.op(eng, [in_], [out], lambda e: e.tensor_copy(out=out.ap, in_=in_.ap))

    def memset(self, out, val, eng="dve"):
        self.op(eng, [], [out], lambda e: e.memset(out.ap, val))

    def recip(self, out, in_):
        self.op("dve", [in_], [out], lambda e: e.reciprocal(out=out.ap, in_=in_.ap))


def build_program(cfg):
    nc = bass.Bass("TRN2", target_bir_lowering=False)
    es = contextlib.ExitStack()
    with es:
        _emit(nc, es, cfg)
    return nc


def _emit(nc, es, cfg):
    P = Prog(nc, es)
    layers = cfg.get("layers", ["A", "F", "B", "F", "A", "F", "B", "F"])

    def din(name, shape):
        return nc.dram_tensor(name, list(shape), F32, kind="ExternalInput").ap()

    def dout(name, shape):
        return nc.dram_tensor(name, list(shape), F32, kind="ExternalOutput").ap()

    xp = din("xp", (SEQ, D))
    xs = din("xs", (NS, D))
    sd = din("sd", (2, NS, H_B, 128, 128))
    sc = din("sc", (2, NS, 3, QKV))
    norm_mix = din("norm_mix", (DEPTH, D))
    norm_ffn = din("norm_ffn", (DEPTH, D))
    norm_final = din("norm_final", (D,))
    a_w_in = din("a_w_in", (2, D, 2 * D_A))
    a_v_norm = din("a_v_norm", (2, D_A))
    a_w_spatial = din("a_w_spatial", (2, H_A, 128, 128))
    a_b_spatial = din("a_b_spatial", (2, H_A, 128))
    a_w_out = din("a_w_out", (2, D_A, D))
    b_w_in = din("b_w_in", (2, D, B_IN))
    b_w_conv = din("b_w_conv", (2, 4, QKV))
    b_a_log = din("b_a_log", (2, H_B))
    b_dt_bias = din("b_dt_bias", (2, H_B))
    b_o_norm = din("b_o_norm", (2, 128))
    b_w_out = din("b_w_out", (2, D, D))
    ffn_w_in = din("ffn_w_in", (DEPTH, D, 2 * D_FF))
    ffn_w_out = din("ffn_w_out", (DEPTH, D_FF, D))

    yp = dout("yp", (SEQ, D))
    ys = dout("ys", (NS, D))
    ndp = dout("ndp", (2, H_B, 128, 128))
    ncp = dout("ncp", (2, 3, QKV))
    nds = dout("nds", (2, NS, H_B, 128, 128))
    ncs = dout("ncs", (2, NS, 3, QKV))
    ncv = dout("ncv", (2, NS, D_A))

    xT = P.alloc(F32, (KC, NT))
    ident_f = P.alloc(F32, (128,))
    ident_b = P.alloc(BF16, (128,))
    ones_b = P.alloc(BF16, (128,))
    ones128_b = P.alloc(BF16, (128,))
    ones_f = P.alloc(F32, (128,))
    epsc = P.alloc(F32, (1,))
    gmix = P.alloc(F32, (DEPTH, KC))
    gffn = P.alloc(F32, (DEPTH, KC))
    NW = 2
    WSLOT = [P.alloc(BF16, (KC, 512)) for _ in range(NW)]
    OSLOT = [P.alloc(BF16, (4, D)) for _ in range(2)]
    wctr = [0]
    octr = [0]

    P.memset(ones_b.full(), 1.0 / D)
    P.memset(ones128_b.full(), 1.0 / 128)
    P.memset(ones_f.full(), 1.0)
    P.memset(epsc.full(), EPS)
    P.memset(ident_f.full(), 0.0)
    P.op("pool", [ones_f.full()], [ident_f.full()],
         lambda g: g.affine_select(out=ident_f.ap, in_=ones_f.ap, pattern=[[-1, 128]], compare_op=ALU.is_equal,
                                   fill=0.0, base=0, channel_multiplier=1))
    P.copy(ident_b.full(), ident_f.full())
    with nc.allow_non_contiguous_dma(reason="tiny gain vectors"):
        P.dma("sp", gmix.full(), norm_mix.rearrange("l (k p) -> p l k", p=128))
        P.dma("sp", gffn.full(), norm_ffn.rearrange("l (k p) -> p l k", p=128))

    def load_w(dram_ap_fn):
        slot = WSLOT[wctr[0] % NW]
        wctr[0] += 1
        dram_ap_fn(slot)
        return slot

    def rmsnorm_T(hT, gain, li, sq, rs):
        for (c0, n) in TILES:
            for k in range(KC):
                P.act(sq[:, k, 0:n], xT[:, k, c0:c0 + n], AF.Square)
            ps = P.bank()
            P.mm(ps[:, 0:n], [(ones_b.full(), sq[:, k, 0:n]) for k in range(KC)])
            P.act(rs[:, 1, 0:n], ps[:, 0:n], AF.Ln, bias=epsc[:, 0:1])
            P.act(rs[:, 1, 0:n], rs[:, 1, 0:n], AF.Exp, scale=-0.5)
            for k in range(KC):
                P.stt(hT[:, k, c0:c0 + n], xT[:, k, c0:c0 + n], gain[:, li, k:k + 1], rs[:, 1, 0:n],
                      ALU.mult, ALU.mult)

    def load_x():
        m = P.mark()
        xin = [P.alloc(F32, (D,)) for _ in range(2)]
        nb = SEQ // 128
        for b in cfg.get('lx_blocks', range(nb + 1)):
            xi = xin[b % 2]
            rows = 128 if b < nb else NS
            src = xp[b * 128:(b + 1) * 128, :] if b < nb else xs
            P.dma("sp", xi[0:rows, :], src)
            for half in range(2):
                ps = P.bank()
                for kk in range(4):
                    k = half * 4 + kk
                    P.transpose_hw(ps[:, kk * 128:kk * 128 + rows], xi[0:rows, k * 128:(k + 1) * 128],
                                ident_f[0:rows, 0:rows])
                for kk in range(4):
                    k = half * 4 + kk
                    eng = "act" if (kk % 2 and not cfg.get("lx_noact")) else "dve"
                    P.copy(xT[:, k, b * 128:b * 128 + rows], ps[:, kk * 128:kk * 128 + rows], eng=eng)
        P.release(m)

    def final_out():
        m = P.mark()
        gfin = P.alloc(F32, (D,))
        P.dma("sp", gfin.full(), norm_final.partition_broadcast(128))
        sqt = P.alloc(F32, (D,))
        st = P.alloc(F32, (4,))
        yo = [P.alloc(F32, (D,)) for _ in range(2)]
        nb = SEQ // 128
        for b in range(nb + 1):
            rows = 128 if b < nb else NS
            ps = P.bank2()
            for k in range(KC):
                P.transpose_hw(ps[0:rows, k * 128:(k + 1) * 128], xT[:, k, b * 128:b * 128 + rows], ident_f.full())
            P.act(sqt[0:rows, :], ps[0:rows, :], AF.Square)
            P.op("dve", [sqt[0:rows, :]], [st[0:rows, 0:1]],
                 lambda e: e.reduce_sum(out=st[0:rows, 0:1].ap, in_=sqt[0:rows, :].ap, axis=AX.X))
            P.act(st[0:rows, 1:2], st[0:rows, 0:1], AF.Sqrt, bias=epsc[0:rows, 0:1], scale=1.0 / D)
            P.recip(st[0:rows, 2:3], st[0:rows, 1:2])
            y = yo[b % 2]
            P.stt(y[0:rows, :], ps[0:rows, :], st[0:rows, 2:3], gfin[0:rows, :], ALU.mult, ALU.mult)
            dst = yp[b * 128:(b + 1) * 128, :] if b < nb else ys
            P.dma("sp", dst, y[0:rows, :])
        P.release(m)

    def ffn(li):
        m = P.mark()
        WS = WSLOT + [P.alloc(BF16, (KC, 512))]
        hT = P.alloc(BF16, (KC, NT))
        sq = P.alloc(BF16, (KC, 512))
        rs = P.alloc(F32, (2, 512))
        rmsnorm_T(hT, gffn, li, sq, rs)
        actb = P.alloc(BF16, (4, NT))
        gs = [P.alloc(F32, (512,)) for _ in range(3)]
        gsc = 0
        win = ffn_w_in[li].rearrange("(k p) n -> p k n", p=128)
        wout = ffn_w_out[li].rearrange("(j p) n -> p j n", p=128)
        quarters = [(0, 4), (4, 4), (8, 4), (12, 4), (16, 4), (20, 2)]

        def issue_w(sb):
            slot = WS[wctr[0] % 3]
            wctr[0] += 1
            c = sb * 256
            P.dma("pool", slot[:, :, 0:256], win[:, :, c:c + 256])
            P.dma("pool", slot[:, :, 256:512], win[:, :, D_FF + c:D_FF + c + 256])
            return slot

        def issue_o(q0, nf):
            slot = OSLOT[octr[0] % 2]
            octr[0] += 1
            P.dma("pool", slot[:, 0:nf, :], wout[:, q0:q0 + nf, :])
            return slot

        sbs = [issue_w(0)]
        oslots = [issue_o(*quarters[0])]
        for qi, (q0, nf) in enumerate(quarters):
            for s in range(nf // 2):
                sb = q0 // 2 + s
                if sb + 1 < 11:
                    sbs.append(issue_w(sb + 1))
                w = sbs[sb]
                for j in range(2):
                    jj = s * 2 + j
                    for (c0, n) in TILES:
                        pg = P.bank()
                        pu = P.bank()
                        P.mm(pg[:, 0:n], [(w[:, k, j * 128:(j + 1) * 128], hT[:, k, c0:c0 + n]) for k in range(KC)])
                        P.mm(pu[:, 0:n], [(w[:, k, 256 + j * 128:256 + (j + 1) * 128], hT[:, k, c0:c0 + n])
                                          for k in range(KC)])
                        g = gs[gsc % 3]
                        gsc += 1
                        P.act(g[:, 0:n], pg[:, 0:n], AF.Silu)
                        P.tt(actb[:, jj, c0:c0 + n], g[:, 0:n], pu[:, 0:n], ALU.mult)
            if qi + 1 < len(quarters):
                oslots.append(issue_o(*quarters[qi + 1]))
            wo = oslots[qi]
            for (c0, n) in TILES:
                for d in range(KC):
                    ps = P.bank()
                    P.mm(ps[:, 0:n], [(wo[:, jj, d * 128:(d + 1) * 128], actb[:, jj, c0:c0 + n]) for jj in range(nf)])
                    P.tt(xT[:, d, c0:c0 + n], xT[:, d, c0:c0 + n], ps[:, 0:n], ALU.add)
        P.release(m)

    def rmsnorm_tile(hTt, gain, li, sq, rs, c0, n, o0=0):
        for k in range(KC):
            P.act(sq[:, k, 0:n], xT[:, k, c0:c0 + n], AF.Square)
        ps = P.bank()
        P.mm(ps[:, 0:n], [(ones_b.full(), sq[:, k, 0:n]) for k in range(KC)])
        P.act(rs[:, 0, 0:n], ps[:, 0:n], AF.Ln, bias=epsc[:, 0:1])
        P.act(rs[:, 0, 0:n], rs[:, 0, 0:n], AF.Exp, scale=-0.5)
        for k in range(KC):
            P.stt(hTt[:, k, o0:o0 + n], xT[:, k, c0:c0 + n], gain[:, li, k:k + 1], rs[:, 0, 0:n],
                  ALU.mult, ALU.mult)

    one_row = P.alloc(BF16, (128,))
    P.memset(one_row.full(), 1.0)

    def mixer_a(li, j):
        m = P.mark()
        WS = WSLOT + [P.alloc(BF16, (KC, 512))]
        NTT = 512 + NS
        hTt = P.alloc(BF16, (KC, NTT))
        rs = P.alloc(F32, (2, 512))
        uT = P.alloc(BF16, (16, NTT))
        sq = Region(P.arena, "S", uT.lo, BF16, (KC, 512))
        vtok = P.alloc(BF16, (4, D_A))
        vn = [P.alloc(BF16, (D_A,)) for _ in range(2)]
        gvb = P.alloc(F32, (D_A,))
        WT = P.alloc(BF16, (H_A, 128))
        wsf = Region(P.arena, "S", vtok.lo, F32, (H_A, 128))
        Wsamp = P.alloc(BF16, (H_A, 16))
        w00 = P.alloc(F32, (H_A,))
        b0col = P.alloc(F32, (H_A,))
        browf = Region(P.arena, "S", vtok.lo + 4096, F32, (H_A * 128,))
        brow = P.alloc(BF16, (H_A, 128))
        sqs = P.alloc(F32, (512,))
        st = P.alloc(F32, (8,))
        sqss = [sqs, P.alloc(F32, (512,))]
        sts = [P.alloc(F32, (8,)) for _ in range(2)]
        vs = P.alloc(F32, (D_A,))
        vsn = P.alloc(F32, (D_A,))
        P.dma("sp", wsf.full(), a_w_spatial[j].rearrange("g t s -> t g s"))
        P.dma("sp", gvb.full(), a_v_norm[j].partition_broadcast(128))
        with nc.allow_non_contiguous_dma(reason="tiny per-group scalars"):
            P.dma("sp", w00[0:16, :], a_w_spatial[j, :, 0, 0].partition_broadcast(16))
            P.dma("sp", b0col.full(), a_b_spatial[j, :, 0].partition_broadcast(128))
        P.dma("sp", browf[0:1, :], a_b_spatial[j].rearrange("g t -> (g t)").partition_broadcast(1))
        P.copy(brow[0:1, :, :], Region(P.arena, "S", browf.lo, F32, (H_A, 128))[0:1, :, :])
        P.op("pool", [wsf.full()], [wsf.full()],
             lambda g: g.affine_select(out=wsf.ap, in_=wsf.ap, pattern=[[0, H_A], [-1, 128]],
                                       compare_op=ALU.is_ge, fill=0.0, base=0, channel_multiplier=1))
        for half in range(2):
            ps = P.bank()
            for gg in range(4):
                P.transpose(ps[:, gg * 128:(gg + 1) * 128], wsf[:, half * 4 + gg, :], ident_f.full())
            P.copy(WT[:, half * 4:half * 4 + 4, :], Region(P.psum, "P", ps.lo, F32, (4, 128)).full())
        for g in range(H_A):
            P.ts(Wsamp[0:16, g, :], ident_f[0:16, 0:16], w00[0:16, g:g + 1], ALU.mult)
        win = a_w_in[j].rearrange("(k p) n -> p k n", p=128)
        wout = a_w_out[j].rearrange("(f p) n -> p f n", p=128)

        for tg in range(4):
            c0 = tg * 512
            subs = [(c0, 512, 0)] + ([(SEQ, NS, 512)] if tg == 3 else [])
            for (cc, n, o0) in subs:
                rmsnorm_tile(hTt, gmix, li, sq, rs, cc, n, o0)
            for fb in range(4):
                slot = WS[wctr[0] % 3]
                wctr[0] += 1
                P.dma("pool", slot.full(), win[:, :, fb * 512:(fb + 1) * 512])
                for fc in range(4):
                    for (cc, n, o0) in subs:
                        ps = P.bank()
                        P.mm(ps[:, 0:n], [(slot[:, k, fc * 128:(fc + 1) * 128], hTt[:, k, o0:o0 + n])
                                          for k in range(KC)])
                        P.act(uT[:, fb * 4 + fc, o0:o0 + n], ps[:, 0:n], AF.Gelu_apprx_tanh)
            for fb in range(4):
                slot = WS[wctr[0] % 3]
                wctr[0] += 1
                P.dma("pool", slot.full(), win[:, :, D_A + fb * 512:D_A + (fb + 1) * 512])
                for c in range(4):
                    ps = P.bank()
                    P.mm(ps.full(), [(hTt[:, k, c * 128:(c + 1) * 128], slot[:, k, :]) for k in range(KC)])
                    P.act(vtok[:, c, fb * 512:(fb + 1) * 512], ps.full(), AF.Gelu_apprx_tanh)
                if tg == 3:
                    ps = P.bank()
                    P.mm(ps[0:NS, :], [(hTt[:, k, 512:512 + NS], slot[:, k, :]) for k in range(KC)])
                    P.act(vs[0:NS, fb * 512:(fb + 1) * 512], ps[0:NS, :], AF.Gelu_apprx_tanh)
            def prep(c):
                stc = sts[c % 2]
                for fb in range(4):
                    sq_ = sqss[fb % 2]
                    P.act(sq_.full(), vtok[:, c, fb * 512:(fb + 1) * 512], AF.Square)
                    P.op("dve", [sq_.full()], [stc[:, fb:fb + 1]],
                         lambda e, fb=fb, sq_=sq_, stc=stc: e.reduce_sum(out=stc[:, fb:fb + 1].ap, in_=sq_.ap, axis=AX.X))
                P.op("dve", [stc[:, 0:4]], [stc[:, 4:5]],
                     lambda e, stc=stc: e.reduce_sum(out=stc[:, 4:5].ap, in_=stc[:, 0:4].ap, axis=AX.X))
                P.act(stc[:, 5:6], stc[:, 4:5], AF.Sqrt, bias=epsc[:, 0:1], scale=1.0 / D_A)
                P.recip(stc[:, 6:7], stc[:, 5:6])
                P.stt(vn[c % 2].full(), vtok[:, c, :], stc[:, 6:7], gvb.full(), ALU.mult, ALU.mult)

            def mixp(c):
                v_n = vn[c % 2]
                for q in range(4):
                    ps = P.bank()
                    for ff in range(4):
                        fc = q * 4 + ff
                        g = fc // 2
                        P.mm(ps[:, ff * 128:(ff + 1) * 128],
                             [(v_n[:, fc * 128:(fc + 1) * 128], WT[:, g, :]),
                              (one_row[0:1, 0:128], brow[0:1, g, :])])
                    P.tt(uT[:, q * 4:q * 4 + 4, c * 128:(c + 1) * 128],
                         Region(P.psum, "P", ps.lo, F32, (4, 128)).full(),
                         uT[:, q * 4:q * 4 + 4, c * 128:(c + 1) * 128], ALU.mult)

            prep(0)
            for c in range(4):
                if c + 1 < 4:
                    prep(c + 1)
                mixp(c)
            if tg == 3:
                for fb in range(4):
                    P.act(sqs[0:NS, :], vs[0:NS, fb * 512:(fb + 1) * 512], AF.Square)
                    P.op("dve", [sqs[0:NS, :]], [st[0:NS, fb:fb + 1]],
                         lambda e, fb=fb: e.reduce_sum(out=st[0:NS, fb:fb + 1].ap, in_=sqs[0:NS, :].ap, axis=AX.X))
                P.op("dve", [st[0:NS, 0:4]], [st[0:NS, 4:5]],
                     lambda e: e.reduce_sum(out=st[0:NS, 4:5].ap, in_=st[0:NS, 0:4].ap, axis=AX.X))
                P.act(st[0:NS, 5:6], st[0:NS, 4:5], AF.Sqrt, bias=epsc[0:NS, 0:1], scale=1.0 / D_A)
                P.recip(st[0:NS, 6:7], st[0:NS, 5:6])
                P.stt(vsn[0:NS, :], vs[0:NS, :], st[0:NS, 6:7], gvb[0:NS, :], ALU.mult, ALU.mult)
                P.dma("sp", ncv[j], vsn[0:NS, :])
                v_n = vn[0]
                P.copy(v_n[0:NS, :], vsn[0:NS, :])
                for q in range(4):
                    ps = P.bank()
                    for ff in range(4):
                        fc = q * 4 + ff
                        g = fc // 2
                        P.mm(ps[:, ff * 128:ff * 128 + NS], [(v_n[0:NS, fc * 128:(fc + 1) * 128], Wsamp[0:NS, g, :])])
                    for ff in range(4):
                        fc = q * 4 + ff
                        g = fc // 2
                        P.stt(uT[:, fc, 512:512 + NS], ps[:, ff * 128:ff * 128 + NS], b0col[:, g:g + 1],
                              uT[:, fc, 512:512 + NS], ALU.add, ALU.mult)
            for (f0, nf) in ((0, 4), (4, 4), (8, 4), (12, 4)):
                wo = OSLOT[octr[0] % 2]
                octr[0] += 1
                P.dma("pool", wo[:, 0:nf, :], wout[:, f0:f0 + nf, :])
                for (cc, n, o0) in subs:
                    for d in range(KC):
                        ps = P.bank()
                        P.mm(ps[:, 0:n], [(wo[:, jj, d * 128:(d + 1) * 128], uT[:, f0 + jj, o0:o0 + n])
                                          for jj in range(nf)])
                        P.tt(xT[:, d, cc:cc + n], xT[:, d, cc:cc + n], ps[:, 0:n], ALU.add)
        P.release(m)

    def R4(ps):
        return Region(P.psum, "P", ps.lo, F32, (4, 128))

    def mixer_b(li, l):
        m = P.mark()
        WSB = WSLOT + [P.alloc(BF16, (KC, 512))]
        BT = 512
        NCH = BT // 128
        NTG = SEQ // BT
        NTT = BT + NS
        hTt = P.alloc(BF16, (KC, NTT))
        rs = P.alloc(F32, (1, BT))
        qn = P.alloc(BF16, (H_B, NTT))
        kn = P.alloc(BF16, (H_B, NTT))
        vT = P.alloc(BF16, (H_B, NTT))
        gateS = Region(P.arena, "S", qn.lo, BF16, (H_B, NTT))
        onT = Region(P.arena, "S", kn.lo, BF16, (H_B, NTT))
        oT = P.alloc(BF16, (H_B, NTT))
        sq = Region(P.arena, "S", oT.lo, BF16, (KC, BT))
        rbb = [P.alloc(BF16, (BT + 4,)) for _ in range(2)]
        dg = [P.alloc(BF16, (4, 128)) for _ in range(2)]
        acc = [P.alloc(BF16, (BT,)) for _ in range(2)]
        accS = [P.alloc(F32, (NS,)) for _ in range(2)]
        sqh = P.alloc(BF16, (BT,))
        halo = P.alloc(BF16, (24, 3))
        wconv = P.alloc(F32, (24, 4))
        wc4 = Region(P.arena, "S", qn.lo, F32, (QKV,))
        nA = P.alloc(F32, (H_B,))
        dtb = P.alloc(F32, (H_B,))
        gocol = P.alloc(F32, (1,))
        Utri = P.alloc(F32, (128,))
        Lmask = P.alloc(BF16, (128,))
        Amask = P.alloc(BF16, (128,))
        onesb4 = P.alloc(BF16, (128,))
        batok = P.alloc(F32, (NCH, 16))
        sc8s = [P.alloc(F32, (17, H_B)) for _ in range(2)]
        sc8 = sc8s[0]
        class CB:
            pass
        CBS = []
        for _ci in range(2):
            B = CB()
            B.base = P.top
            B.gbc = P.alloc(F32, (4, 128))
            B.dtmp = Region(P.arena, "S", B.gbc.lo, F32, (4, 128))
            B.Dm = P.alloc(BF16, (4, 128))
            B.Dms = P.alloc(BF16, (4, 128))
            B.egT = P.alloc(BF16, (4, 128))
            B.qg = P.alloc(BF16, (4, 128))
            B.L = P.alloc(F32, (4, 128))
            B.U = P.alloc(F32, (4, 128))
            B.Pm = P.alloc(F32, (4, 128))
            B.Am = P.alloc(BF16, (4, 128))
            B.TmT = Region(P.arena, "S", B.Am.lo, BF16, (4, 128))
            B.AT = Region(P.arena, "S", B.Dm.lo, BF16, (4, 128))
            B.kbg = P.alloc(BF16, (4, 128))
            B.kd = P.alloc(BF16, (4, 128))
            B.vb = P.alloc(BF16, (4, 128))
            B.utok = P.alloc(BF16, (4, 128))
            B.wT = Region(P.arena, "S", B.vb.lo, BF16, (4, 128))
            B.vnew = B.utok
            CBS.append(B)
        S_f = P.alloc(F32, (H_B, 128))
        S_b = P.alloc(BF16, (H_B, 128))
        rawS = Region(P.arena, "S", OSLOT[0].lo, F32, (24, NS))
        scT = Region(P.arena, "S", OSLOT[0].lo + 2048, F32, (24, 3, NS))
        assert 2048 + scT.nbytes <= OSLOT[0].nbytes
        sct = Region(P.arena, "S", CBS[1].base, F32, (3, 512))
        rowS = Region(P.arena, "S", CBS[0].base + 12288, F32, (512,))
        kq2 = P.alloc(F32, (H_B, NS, 2))
        bas = P.alloc(F32, (16,))
        decb = Region(P.arena, "S", CBS[0].base + 14336, F32, (NS, H_B))
        betab = Region(P.arena, "S", CBS[0].base + 14336 + 512, F32, (NS, H_B))
        kqb = Region(P.arena, "S", CBS[0].base + 14336 + 1024, F32, (H_B, NS))
        rhsd = Region(P.arena, "S", CBS[0].base + 14336 + 1536, F32, (NS, H_B))
        Sin = [Region(P.arena, "S", CBS[0].base + i * 4096, F32, (H_B, 128)) for i in range(2)]
        Snew = Region(P.arena, "S", CBS[0].base + 8192, F32, (H_B, 128))
        dSs = [P.alloc(F32, (6, H_B)) for _ in range(2)]
        Snews = [Snew, Region(P.arena, "S", CBS[1].base + 6144, F32, (H_B, 128))]
        dbc4 = Region(P.arena, "S", CBS[1].base + 10240, F32, (4, 128))
        P.dma("sp", wc4[0:4, :], b_w_conv[l])
        P.dma("sp", nA.full(), b_a_log[l].partition_broadcast(128))
        P.dma("sp", dtb.full(), b_dt_bias[l].partition_broadcast(128))
        with nc.allow_non_contiguous_dma(reason="128-element column"):
            P.dma("sp", gocol.full(), b_o_norm[l].rearrange("(p o) -> p o", o=1))
        ps = P.bank()
        for fc in range(24):
            P.transpose(ps[:, fc * 4:fc * 4 + 4], wc4[0:4, fc * 128:(fc + 1) * 128], ident_f[0:4, 0:4])
        P.copy(wconv.full(), Region(P.psum, "P", ps.lo, F32, (24, 4)).full())
        P.act(nA.full(), nA.full(), AF.Exp)
        P.ts(nA.full(), nA.full(), -1.0, ALU.mult)
        P.op("pool", [ones_f.full()], [Utri.full()],
             lambda g: g.affine_select(out=Utri.ap, in_=ones_f.ap, pattern=[[1, 128]], compare_op=ALU.is_ge,
                                       fill=0.0, base=0, channel_multiplier=-1))
        P.memset(onesb4.full(), 1.0)
        P.op("pool", [onesb4.full()], [Lmask.full()],
             lambda g: g.affine_select(out=Lmask.ap, in_=onesb4.ap, pattern=[[-1, 128]],
                                       compare_op=ALU.is_gt, fill=0.0, base=0, channel_multiplier=1))
        P.op("pool", [onesb4.full()], [Amask.full()],
             lambda g: g.affine_select(out=Amask.ap, in_=onesb4.ap, pattern=[[-1, 128]],
                                       compare_op=ALU.is_ge, fill=0.0, base=0, channel_multiplier=1))
        P.memset(halo.full(), 0.0)
        P.memset(S_f.full(), 0.0)
        P.memset(S_b.full(), 0.0)
        win = b_w_in[l].rearrange("(k p) n -> p k n", p=128)
        wout = b_w_out[l].rearrange("(f p) n -> p f n", p=128)
        rbc = [0]
        sqh2 = Region(P.arena, "S", rbb[0].lo, BF16, (BT,))
        rs2 = Region(P.arena, "S", dg[0].lo, F32, (1, BT))
        assert dg[1].lo == dg[0].lo + 1024 and rs2.nbytes <= 2048
        nrm = [(sqh, rs), (sqh2, rs2)]
        nrc = [0]

        def conv_silu(psr, fcg, n, dst):
            rb = rbb[rbc[0] % 2]
            d = dg[rbc[0] % 2]
            rbc[0] += 1
            P.copy(rb[:, 3:3 + n], psr)
            P.copy(rb[:, 0:3], halo[:, fcg, :])
            for jt in range(4):
                P.ts(d[:, jt, :], ident_b.full(), wconv[:, fcg, jt:jt + 1], ALU.mult)
            ps2 = P.bank()
            P.mm(ps2[:, 0:n], [(d[:, jt, :], rb[:, jt:jt + n]) for jt in range(4)])
            P.copy(halo[:, fcg, :], rb[:, n:n + 3])
            P.act(dst, ps2[:, 0:n], AF.Silu)

        def rsqrt_act(out, in_, np_=128):
            P.act(out, in_, AF.Ln, bias=epsc[0:np_, 0:1])
            P.act(out, out, AF.Exp, scale=-0.5)

        def l2n(src, dst, n, scale):
            sq_, rs_ = nrm[nrc[0] % 2]
            nrc[0] += 1
            P.act(sq_[:, 0:n], src, AF.Square)
            ps = P.bank()
            P.mm(ps[:, 0:n], [(one_row.full(), sq_[:, 0:n])])
            rsqrt_act(rs_[:, 0, 0:n], ps[:, 0:n])
            P.stt(dst, src, scale, rs_[:, 0, 0:n], ALU.mult, ALU.mult)

        def gate_decay(sc, np_, a_raw, gout):
            r = lambda i: sc[0:np_, i, :]
            P.tt(r(10), a_raw, dtb[0:np_, :], ALU.add)
            P.act(r(11), r(10), AF.Exp)
            P.act(r(12), r(11), AF.Ln, bias=1.0)
            P.ts(r(13), r(11), 2.0, ALU.add)
            P.recip(r(13), r(13))
            P.tt(r(13), r(13), r(11), ALU.mult)
            P.tt(r(14), r(13), r(13), ALU.mult)
            P.ts(r(15), r(14), 1.0 / 9, ALU.mult, 1.0 / 7, ALU.add)
            P.tt(r(15), r(15), r(14), ALU.mult)
            P.ts(r(15), r(15), 1.0 / 5, ALU.add)
            P.tt(r(15), r(15), r(14), ALU.mult)
            P.ts(r(15), r(15), 1.0 / 3, ALU.add)
            P.tt(r(15), r(15), r(14), ALU.mult)
            P.ts(r(15), r(15), 1.0, ALU.add)
            P.tt(r(15), r(15), r(13), ALU.mult)
            P.ts(r(15), r(15), 2.0, ALU.mult)
            P.ts(r(16), r(11), 1.0, ALU.is_le)
            P.tt(r(15), r(15), r(12), ALU.subtract)
            P.tt(r(15), r(15), r(16), ALU.mult)
            P.tt(r(15), r(15), r(12), ALU.add)
            P.tt(gout, r(15), nA[0:np_, :], ALU.mult)

        def sample_delta():
            with nc.allow_non_contiguous_dma(reason="hbm->hbm state row copy"):
                P.dma("sp", ncs[l][:, 0:2, :], sc[l][:, 1:3, :])
            P.dma("sp", Sin[0].full(), sd[l, 0].rearrange("h k v -> k h v"))
            for fcg in range(24):
                a = accS[fcg % 2]
                P.ts(a[:, 0:NS], scT[:, fcg, 0, :], wconv[:, fcg, 0:1], ALU.mult)
                P.stt(a[:, 0:NS], scT[:, fcg, 1, :], wconv[:, fcg, 1:2], a[:, 0:NS], ALU.mult, ALU.add)
                P.stt(a[:, 0:NS], scT[:, fcg, 2, :], wconv[:, fcg, 2:3], a[:, 0:NS], ALU.mult, ALU.add)
                P.stt(a[:, 0:NS], rawS[:, fcg, :], wconv[:, fcg, 3:4], a[:, 0:NS], ALU.mult, ALU.add)
                P.act(rawS[:, fcg, :], a[:, 0:NS], AF.Silu)
            for fcg in range(16):
                h = fcg % 8
                l2n(rawS[:, fcg, :], kq2[:, h, :, 1 if fcg < 8 else 0], NS, 128.0 ** -0.5 if fcg < 8 else 1.0)
            r = lambda i: sc8[0:NS, i, :]
            P.act(r(0), bas[0:NS, 0:8], AF.Exp, scale=-1.0)
            P.ts(r(0), r(0), 1.0, ALU.add)
            P.recip(r(0), r(0))
            gate_decay(sc8, NS, bas[0:NS, 8:16], r(2))
            P.act(r(3), r(2), AF.Exp)
            for (src, dst) in ((r(3), decb), (r(0), betab)):
                for t in range(NS):
                    P.ts(rhsd[0:NS, t, :], src, ident_f[0:NS, t:t + 1], ALU.mult)
                ps = P.bank()
                P.mm(ps[:, 0:NS * H_B], [(ones_f[0:NS, 0:128], Region(P.arena, "S", rhsd.lo, F32, (NS * H_B,))[0:NS, :])])
                P.copy(dst.full(), Region(P.psum, "P", ps.lo, F32, (NS, H_B)).full())
            P.tt(kqb.full(), kq2[:, :, :, 0], kq2[:, :, :, 1], ALU.mult)
            ps = P.bank()
            P.mm(ps[:, 0:H_B * NS], [(ones_f.full(), Region(P.arena, "S", kqb.lo, F32, (H_B * NS,)).full())])
            P.copy(kqb.full(), Region(P.psum, "P", ps.lo, F32, (H_B, NS)).full())
            for t in range(NS):
                Si = Sin[t % 2]
                Sn = Snews[t % 2]
                dS = dSs[t % 2]
                if t + 1 < NS:
                    P.dma("sp", Sin[(t + 1) % 2].full(), sd[l, t + 1].rearrange("h k v -> k h v"))
                ps = P.bank()
                for h in range(H_B):
                    P.mm(ps[:, h * 2:h * 2 + 2], [(Si[:, h, :], kq2[:, h, t, :])])
                pv2 = Region(P.psum, "P", ps.lo, F32, (H_B, 2))
                d_ = lambda i: dS[:, i, :]
                P.tt(d_(0), pv2[:, :, 0], decb[:, t, :], ALU.mult)
                P.tt(d_(0), rawS[:, 16:24, t], d_(0), ALU.subtract)
                P.tt(d_(0), d_(0), betab[:, t, :], ALU.mult)
                P.tt(d_(1), pv2[:, :, 1], decb[:, t, :], ALU.mult)
                P.tt(d_(2), d_(0), kqb[:, :, t], ALU.mult)
                P.tt(oT[:, :, BT + t], d_(1), d_(2), ALU.add)
                for half in range(2):
                    pb = P.bank()
                    dv = dS[:, 0, half * 4:half * 4 + 4]
                    P.tt(dbc4.full(), bcm4(ident_f),
                         V(dv.ap.unsqueeze(2).broadcast_to([128, 4, 128]), dv.space, dv.lo, dv.hi), ALU.mult)
                    P.mm(pb.full(), [(ones_f.full(), Region(P.arena, "S", dbc4.lo, F32, (512,)).full())])
                    for i in range(4):
                        h = half * 4 + i
                        P.act(Sn[:, h, :], Si[:, h, :], AF.Identity, scale=decb[:, t, h:h + 1])
                        P.stt(Sn[:, h, :], R4(pb)[:, i, :], kq2[:, h, t, 0:1], Sn[:, h, :], ALU.mult, ALU.add)
                P.dma("sp", nds[l, t].rearrange("h k v -> k h v"), Sn.full())

        def chain(cs, hh, B, sc8):
            hs = [hh * 4 + i for i in range(4)]
            P.tt(B.gbc.full(), bcm4(Utri), bc4(sc8, 2, hs[0]), ALU.mult)
            psg = P.bank()
            P.mm(psg.full(), [(ones_f.full(), Region(P.arena, "S", B.gbc.lo, F32, (512,)).full())])
            yield
            P.tt(B.dtmp.full(), R4(psg).full(), bc4(sc8, 3, hs[0]), ALU.subtract)
            P.ts(B.dtmp.full(), B.dtmp.full(), 0.0, ALU.max)
            P.act(B.egT.full(), R4(psg).full(), AF.Exp)
            P.act(B.Dm.full(), B.dtmp.full(), AF.Exp, scale=-1.0)
            P.tt(B.Dms.full(), B.Dm.full(), bcm4(Lmask), ALU.mult)
            P.tt(B.Dms.full(), B.Dms.full(), bc4(sc8, 0, hs[0]), ALU.mult)
            P.tt(B.Dm.full(), B.Dm.full(), bcm4(Amask), ALU.mult)
            P.tt(B.qg.full(), qn[:, hs[0]:hs[0] + 4, cs], B.egT.full(), ALU.mult)
            pkk = P.bank()
            for i, h in enumerate(hs):
                P.mm(R4(pkk)[:, i, :], [(kn[:, h, cs], kn[:, h, cs])])
            pqk = P.bank()
            for i, h in enumerate(hs):
                P.mm(R4(pqk)[:, i, :], [(qn[:, h, cs], kn[:, h, cs])])
            yield
            P.tt(B.L.full(), R4(pkk).full(), B.Dms.full(), ALU.mult)
            P.tt(B.Am.full(), R4(pqk).full(), B.Dm.full(), ALU.mult)
            pu = P.bank()
            for i in range(4):
                P.transpose_hw(R4(pu)[:, i, :], B.L[:, i, :], ident_f.full())
            pa = P.bank()
            for i in range(4):
                P.transpose(R4(pa)[:, i, :], B.Am[:, i, :], ident_b.full())
            yield
            P.copy(B.U.full(), R4(pu).full(), eng="act")
            P.tt(B.Pm.full(), bcm4(ident_b), R4(pu).full(), ALU.subtract)
            P.copy(B.AT.full(), R4(pa).full(), eng="act")
            for stp in range(6):
                p1 = P.bank()
                for i in range(4):
                    P.mm(R4(p1)[:, i, :], [(B.U[:, i, :], B.L[:, i, :])])
                yield
                P.copy(B.L.full(), R4(p1).full(), eng="act")
                p3 = P.bank()
                for i in range(4):
                    P.mm(R4(p3)[:, i, :], [(B.L[:, i, :], B.Pm[:, i, :])])
                if stp < 5:
                    p2 = P.bank()
                    for i in range(4):
                        P.transpose_hw(R4(p2)[:, i, :], B.L[:, i, :], ident_f.full())
                yield
                P.tt(B.Pm.full(), B.Pm.full(), R4(p3).full(), ALU.add)
                if stp < 5:
                    P.copy(B.U.full(), R4(p2).full())
            P.copy(B.TmT.full(), B.Pm.full(), eng="act")
            pk = P.bank()
            for i, h in enumerate(hs):
                P.transpose(R4(pk)[:, i, :], kn[:, h, cs], ident_b.full())
            pv = P.bank()
            for i, h in enumerate(hs):
                P.transpose(R4(pv)[:, i, :], vT[:, h, cs], ident_b.full())
            yield
            P.tt(B.kbg.full(), R4(pk).full(), bc4(sc8, 7, hs[0]), ALU.mult)
            P.tt(B.kd.full(), R4(pk).full(), bc4(sc8, 6, hs[0]), ALU.mult)
            P.tt(B.vb.full(), R4(pv).full(), bc4(sc8, 0, hs[0]), ALU.mult)
            pu2 = P.bank()
            for i in range(4):
                P.mm(R4(pu2)[:, i, :], [(B.TmT[:, i, :], B.vb[:, i, :])])
            pw = P.bank()
            for i in range(4):
                P.mm(R4(pw)[:, i, :], [(B.kbg[:, i, :], B.TmT[:, i, :])])
            yield
            P.copy(B.utok.full(), R4(pu2).full(), eng="act")
            P.copy(B.wT.full(), R4(pw).full())
            pws = P.bank()
            for i, h in enumerate(hs):
                P.mm(R4(pws)[:, i, :], [(B.wT[:, i, :], S_b[:, h, :])])
            yield
            P.tt(B.vnew.full(), B.utok.full(), R4(pws).full(), ALU.subtract)
            po = P.bank()
            for i, h in enumerate(hs):
                P.mm(R4(po)[:, i, :], [(S_b[:, h, :], B.qg[:, i, :]), (B.vnew[:, i, :], B.AT[:, i, :])])
            pS = P.bank()
            for i, h in enumerate(hs):
                P.mm(R4(pS)[:, i, :], [(B.kd[:, i, :], B.vnew[:, i, :])])
            yield
            P.copy(oT[:, hs[0]:hs[0] + 4, cs], R4(po).full(), eng="act")
            P.tt(S_f[:, hs[0]:hs[0] + 4, :], S_f[:, hs[0]:hs[0] + 4, :], bc4(sc8, 5, hs[0]), ALU.mult)
            P.tt(S_f[:, hs[0]:hs[0] + 4, :], S_f[:, hs[0]:hs[0] + 4, :], R4(pS).full(), ALU.add)
            P.copy(S_b[:, hs[0]:hs[0] + 4, :], S_f[:, hs[0]:hs[0] + 4, :], eng="act")

        for tg in range(NTG):
            c0 = tg * BT
            last = (tg == NTG - 1)
            rmsnorm_tile(hTt, gmix, li, sq, rs, c0, BT, 0)
            if last:
                rmsnorm_tile(hTt, gmix, li, sq, rs, SEQ, NS, BT)
            pend = [None]
            for fb in range(6):
                slot = WSB[wctr[0] % 3]
                wctr[0] += 1
                P.dma("pool", slot.full(), win[:, :, fb * 512:(fb + 1) * 512])
                for fc in range(4):
                    fcg = fb * 4 + fc
                    ps = P.bank()
                    P.mm(ps[:, 0:BT], [(slot[:, k, fc * 128:(fc + 1) * 128], hTt[:, k, 0:BT]) for k in range(KC)])
                    h = fcg % 8
                    if pend[0] is not None:
                        pend[0]()
                    pend[0] = (lambda ps=ps, fcg=fcg, h=h, fb=fb:
                               conv_silu(ps[:, 0:BT], fcg, BT, (qn if fb < 2 else kn if fb < 4 else vT)[:, h, 0:BT]))
                if fb == 5:
                    pend[0]()
                    pend[0] = None
                if last:
                    ps = P.bank()
                    P.mm(ps[0:3, :], [(hTt[:, k, BT - 3:BT], slot[:, k, :]) for k in range(KC)])
                    P.copy(rowS[0:3, :], ps[0:3, :])
                    P.dma("sp", ncp[l][:, fb * 512:(fb + 1) * 512], rowS[0:3, :])
                    ps = P.bank()
                    for fc in range(4):
                        P.mm(ps[:, fc * NS:(fc + 1) * NS],
                             [(slot[:, k, fc * 128:(fc + 1) * 128], hTt[:, k, BT:BT + NS]) for k in range(KC)])
                    P.copy(rawS[:, fb * 4:fb * 4 + 4, :], Region(P.psum, "P", ps.lo, F32, (4, NS)).full())
                    ps = P.bank()
                    P.mm(ps[0:NS, :], [(hTt[:, k, BT:BT + NS], slot[:, k, :]) for k in range(KC)])
                    P.copy(rowS[0:NS, :], ps[0:NS, :])
                    P.dma("sp", ncs[l][:, 2, fb * 512:(fb + 1) * 512], rowS[0:NS, :])
                    P.dma("sp", sct[0:NS, :, :], sc[l][:, :, fb * 512:(fb + 1) * 512])
                    ps = P.bank()
                    for fc in range(4):
                        for jt in range(3):
                            P.transpose(ps[:, (fc * 3 + jt) * NS:(fc * 3 + jt + 1) * NS],
                                        sct[0:NS, jt, fc * 128:(fc + 1) * 128], ident_f[0:NS, 0:NS])
                    P.copy(scT[:, fb * 4:fb * 4 + 4, :, :], Region(P.psum, "P", ps.lo, F32, (4, 3, NS)).full())
            for (buf, scl) in ((qn, 128.0 ** -0.5), (kn, 1.0)):
                P.act(sq[:, :, 0:BT], buf[:, :, 0:BT], AF.Square)
                for h in range(H_B):
                    ps = P.bank()
                    P.mm(ps[:, 0:BT], [(one_row.full(), sq[:, h, 0:BT])])
                    sq_, rs_ = nrm[nrc[0] % 2]
                    nrc[0] += 1
                    rsqrt_act(rs_[:, 0, 0:BT], ps[:, 0:BT])
                    P.stt(buf[:, h, 0:BT], buf[:, h, 0:BT], scl, rs_[:, 0, 0:BT], ALU.mult, ALU.mult)
            slot = WSB[wctr[0] % 3]
            wctr[0] += 1
            P.dma("pool", slot[:, :, 0:16], win[:, :, 4096:4112])
            for c in range(NCH):
                ps = P.bank()
                P.mm(ps[:, 0:16], [(hTt[:, k, c * 128:(c + 1) * 128], slot[:, k, 0:16]) for k in range(KC)])
                P.copy(batok[:, c, :], ps[:, 0:16])
            if last:
                ps = P.bank()
                P.mm(ps[0:NS, 0:16], [(hTt[:, k, BT:BT + NS], slot[:, k, 0:16]) for k in range(KC)])
                P.copy(bas[0:NS, :], ps[0:NS, 0:16])
            for c in range(NCH):
                cs = slice(c * 128, (c + 1) * 128)
                sc8 = sc8s[c % 2]
                beta = sc8[:, 0, :]
                P.act(beta, batok[:, c, 0:8], AF.Exp, scale=-1.0)
                P.ts(beta, beta, 1.0, ALU.add)
                P.recip(beta, beta)
                gtok = sc8[:, 2, :]
                gate_decay(sc8, 128, batok[:, c, 8:16], gtok)
                ps = P.bank()
                P.mm(ps[:, 0:8], [(Utri.full(), gtok)])
                P.mm(ps[:, 8:16], [(ones_f.full(), gtok)])
                gct = sc8[:, 3, :]
                P.copy(gct, ps[:, 0:8])
                P.act(sc8[:, 4, :], ps[:, 0:8], AF.Exp)
                P.act(sc8[:, 5, :], ps[:, 8:16], AF.Exp)
                P.tt(sc8[:, 6, :], ps[:, 8:16], gct, ALU.subtract)
                P.act(sc8[:, 6, :], sc8[:, 6, :], AF.Exp)
                P.tt(sc8[:, 7, :], beta, sc8[:, 4, :], ALU.mult)
                gens = [chain(cs, hh, CBS[hh], sc8) for hh in range(2)]
                while gens:
                    for g in list(gens):
                        try:
                            next(g)
                        except StopIteration:
                            gens.remove(g)
            if last:
                P.dma("sp", ndp[l].rearrange("h k v -> k h v"), S_f.full())
                sample_delta()
            n = BT
            for fb in (6, 7):
                slot = WSB[wctr[0] % 3]
                wctr[0] += 1
                P.dma("pool", slot.full(), win[:, :, fb * 512:(fb + 1) * 512])
                for fc in range(4):
                    ps = P.bank()
                    P.mm(ps[:, 0:BT], [(slot[:, k, fc * 128:(fc + 1) * 128], hTt[:, k, 0:BT]) for k in range(KC)])
                    P.act(gateS[:, (fb - 6) * 4 + fc, 0:BT], ps[:, 0:BT], AF.Silu)
                    if last:
                        ps = P.bank()
                        P.mm(ps[:, 0:NS], [(slot[:, k, fc * 128:(fc + 1) * 128], hTt[:, k, BT:BT + NS])
                                           for k in range(KC)])
                        P.act(gateS[:, (fb - 6) * 4 + fc, BT:BT + NS], ps[:, 0:NS], AF.Silu)
            subs = [(c0, BT, 0)] + ([(SEQ, NS, BT)] if last else [])
            for h in range(H_B):
                for (cc, n, o0) in subs:
                    sq_, rs_ = nrm[nrc[0] % 2]
                    ac_ = acc[nrc[0] % 2]
                    nrc[0] += 1
                    P.act(sq_[:, 0:n], oT[:, h, o0:o0 + n], AF.Square)
                    ps = P.bank()
                    P.mm(ps[:, 0:n], [(ones128_b.full(), sq_[:, 0:n])])
                    rsqrt_act(rs_[:, 0, 0:n], ps[:, 0:n])
                    P.stt(ac_[:, 0:n], oT[:, h, o0:o0 + n], gocol[:, 0:1], rs_[:, 0, 0:n], ALU.mult, ALU.mult)
                    P.tt(onT[:, h, o0:o0 + n], ac_[:, 0:n], gateS[:, h, o0:o0 + n], ALU.mult)
            for (f0, nf) in ((0, 4), (4, 4)):
                wo = OSLOT[octr[0] % 2]
                octr[0] += 1
                P.dma("pool", wo[:, 0:nf, :], wout[:, f0:f0 + nf, :])
                for (cc, n, o0) in subs:
                    for d in range(KC):
                        ps = P.bank()
                        P.mm(ps[:, 0:n], [(wo[:, jj, d * 128:(d + 1) * 128], onT[:, f0 + jj, o0:o0 + n])
                                          for jj in range(nf)])
                        P.tt(xT[:, d, cc:cc + n], xT[:, d, cc:cc + n], ps[:, 0:n], ALU.add)
        P.release(m)

    def gtok_col(sc8, row, h):
        return sc8[:, row, h:h + 1]

    def bc4(sc8, row, h0):
        v = sc8[:, row, h0:h0 + 4]
        return V(v.ap.unsqueeze(2).broadcast_to([128, 4, 128]), v.space, v.lo, v.hi)

    def bcm4(r):
        v = r.full()
        return V(v.ap.unsqueeze(1).broadcast_to([128, 4, 128]), v.space, v.lo, v.hi)

    def ones4():
        v = ones_f.full()
        return V(v.ap.unsqueeze(1).broadcast_to([128, 4, 128]), v.space, v.lo, v.hi)

    if not cfg.get("skip_load"):
        load_x()
    li_of = {"A": 0, "B": 0, "F": 0}
    lidx = 0
    for kind in layers:
        if kind == "F":
            ffn(cfg.get("ffn_li", [0, 1, 2, 3])[li_of["F"]])
            li_of["F"] += 1
        elif kind == "B":
            mixer_b(2 * li_of["B"] + 1, li_of["B"])
            li_of["B"] += 1
        elif kind == "A":
            mixer_a(2 * li_of["A"], li_of["A"])
            li_of["A"] += 1
    if not cfg.get("skip_final"):
        final_out()
    P.finish()
    cfg["stats"] = dict(nops=P.nops, nwaits=P.nwaits, cnt=dict(P.cnt), dn=dict(P.dn), top=P.top)


def kernel(**inputs):
    cfg = {}
    nc = build_program(cfg)
    in_maps = []
    f = lambda a: np.ascontiguousarray(np.asarray(a, dtype=np.float32))
    shared = {k: f(inputs[k]) for k in (
        "norm_mix", "norm_ffn", "norm_final", "a_w_in", "a_v_norm", "a_w_spatial", "a_b_spatial", "a_w_out",
        "b_w_in", "b_w_conv", "b_a_log", "b_dt_bias", "b_o_norm", "b_w_out", "ffn_w_in", "ffn_w_out")}
    x_prompt = f(inputs["x_prompt"])
    x_sample = f(inputs["x_sample"])
    state_delta = f(inputs["state_delta"])
    state_conv = f(inputs["state_conv"])
    for c in range(NCORES):
        m = dict(shared)
        m["xp"] = np.ascontiguousarray(x_prompt[c])
        m["xs"] = np.ascontiguousarray(x_sample[c * NS:(c + 1) * NS, 0, :])
        m["sd"] = np.ascontiguousarray(state_delta[:, c * NS:(c + 1) * NS])
        m["sc"] = np.ascontiguousarray(state_conv[:, c * NS:(c + 1) * NS])
        in_maps.append(m)
    res = run_bass_kernel_spmd(nc, in_maps, core_ids=list(range(NCORES)))
    R = res.results
    y_prompt = np.stack([R[c]["yp"] for c in range(NCORES)], axis=0)
    y_sample = np.concatenate([R[c]["ys"] for c in range(NCORES)], axis=0)[:, None, :]
    ndp = np.stack([R[c]["ndp"] for c in range(NCORES)], axis=1)
    ncp = np.stack([R[c]["ncp"] for c in range(NCORES)], axis=1)
    nds = np.concatenate([R[c]["nds"] for c in range(NCORES)], axis=1)
    ncs = np.concatenate([R[c]["ncs"] for c in range(NCORES)], axis=1)
    ncv = np.concatenate([R[c]["ncv"] for c in range(NCORES)], axis=1)[:, :, None, :]
    return (y_prompt.astype(np.float32), y_sample.astype(np.float32), ndp.astype(np.float32),
            ncp.astype(np.float32), nds.astype(np.float32), ncs.astype(np.float32), ncv.astype(np.float32))
```
